# Optimizing a Trainium2 kernel written in Bass

```python
import math
import jax
import jax.numpy as jnp
from jax import lax
import numpy as np

D_MODEL = 2048
BATCH = 1
SEQ = 8192
DEPTH = 2

F32 = jnp.float32
RET_HEADS = 4
RET_QK_DIM = 128
RET_V_DIM = 256
RET_CHUNK = 128
ROPE_BASE = 10000.0
GDN_HEADS = 8
GDN_DIM = 128
GDN_CHUNK = 64
CONV_WIDTH = 4
S5_CHANNELS = 1024
S5_GROUP = 16
S5_GROUPS = S5_CHANNELS // S5_GROUP
S5_STATE = 64
S5_STEP_MIN = 1e-3
S5_STEP_MAX = 1e-1
SSD_HEADS = 16
SSD_HEAD_DIM = 64
SSD_INNER = SSD_HEADS * SSD_HEAD_DIM
SSD_GROUPS = 2
SSD_STATE = 128
SSD_CHUNK = 128
DT_MIN = 1e-3
DT_MAX = 1e-1
D_FF = 4 * D_MODEL
N_BRANCH = 4
DEEPNORM_ALPHA = (2 * DEPTH) ** 0.25
DEEPNORM_BETA = (8 * DEPTH) ** -0.25
LN_EPS = 1e-5
RMS_EPS = 1e-6

RET_Q = RET_HEADS * RET_QK_DIM
RET_V = RET_HEADS * RET_V_DIM
GDN_W = GDN_HEADS * GDN_DIM
SSD_BC = SSD_GROUPS * SSD_STATE
SPLIT_SIZES = (
    RET_Q, RET_Q, RET_V, RET_V,
    GDN_W, GDN_W, GDN_W, GDN_W, GDN_HEADS, GDN_HEADS,
    S5_CHANNELS,
    SSD_INNER, SSD_INNER, SSD_BC, SSD_BC, SSD_HEADS,
    N_BRANCH * D_MODEL,
)
D_IN_PROJ = sum(SPLIT_SIZES)
SPLIT_POINTS = tuple(sum(SPLIT_SIZES[:i + 1]) for i in range(len(SPLIT_SIZES) - 1))

kernel_name = 'hybrid_gated_ret_gdn_s5_ssd_block'


def layer_norm(x, g, b):
    xf = x.astype(F32)
    mu = jnp.mean(xf, -1, keepdims=True)
    var = jnp.mean(jnp.square(xf - mu), -1, keepdims=True)
    return ((xf - mu) * lax.rsqrt(var + LN_EPS) * g.astype(F32) + b.astype(F32)).astype(x.dtype)


def rms_norm(x, w):
    xf = x.astype(F32)
    return xf * lax.rsqrt(jnp.mean(xf * xf, -1, keepdims=True) + RMS_EPS) * w.astype(F32)


def l2_norm(x):
    return x * lax.rsqrt(jnp.sum(x * x, -1, keepdims=True) + RMS_EPS)


def causal_conv(x, w):
    rhs = jnp.transpose(w).astype(x.dtype)[:, None, :]
    return lax.conv_general_dilated(x, rhs, window_strides=(1,), padding=[(CONV_WIDTH - 1, 0)],
                                    dimension_numbers=('NWC', 'WIO', 'NWC'),
                                    feature_group_count=x.shape[-1])


def rotary(x, positions):
    half = x.shape[-1] // 2
    inv = ROPE_BASE ** (-jnp.arange(half, dtype=F32) / half)
    ang = positions.astype(F32)[..., None] * inv
    cos = jnp.cos(ang)[:, :, None, :]
    sin = jnp.sin(ang)[:, :, None, :]
    x1, x2 = x[..., :half], x[..., half:]
    return jnp.concatenate([x1 * cos - x2 * sin, x1 * sin + x2 * cos], -1)


def chunk_decay_matrix(cum):
    c = cum.shape[-1]
    idx = jnp.arange(c)
    lower = idx[:, None] >= idx[None, :]
    diff = cum[..., :, None] - cum[..., None, :]
    return jnp.where(lower, jnp.exp(jnp.where(lower, diff, 0.0)), 0.0)


def retention(q, k, v, positions):
    bsz, t, h, dk = q.shape
    dv = v.shape[-1]
    c = RET_CHUNK
    n = t // c
    q = rotary(q.astype(F32), positions)
    k = rotary(k.astype(F32), positions) * dk ** -0.5
    v = v.astype(F32)
    log_g = jnp.log(1.0 - 2.0 ** (-5.0 - jnp.arange(h, dtype=F32)))
    idx = jnp.arange(c, dtype=F32)
    dmat = chunk_decay_matrix(idx[None, :] * log_g[:, None])
    q = q.reshape(bsz, n, c, h, dk)
    k = k.reshape(bsz, n, c, h, dk)
    v = v.reshape(bsz, n, c, h, dv)
    scores = jnp.einsum('bnihd,bnjhd->bnhij', q, k) * dmat
    inner = jnp.einsum('bnhij,bnjhe->bnihe', scores, v)
    k_decay = jnp.exp((c - 1.0 - idx)[:, None] * log_g[None, :])
    chunk_kv = jnp.einsum('bnjhd,jh,bnjhe->bnhde', k, k_decay, v)
    gamma_c = jnp.exp(c * log_g)[None, :, None, None]

    def step(r, kv):
        return r * gamma_c + kv, r

    _, r_prev = lax.scan(step, jnp.zeros((bsz, h, dk, dv), F32), jnp.moveaxis(chunk_kv, 1, 0))
    r_prev = jnp.moveaxis(r_prev, 0, 1)
    q_decay = jnp.exp((idx + 1.0)[:, None] * log_g[None, :])
    cross = jnp.einsum('bnihd,ih,bnhde->bnihe', q, q_decay, r_prev)
    return (inner + cross).reshape(bsz, t, h, dv)


def retention_mixer(rq, rk, rv, rg, positions):
    bsz, t, _ = rq.shape
    y = retention(rq.reshape(bsz, t, RET_HEADS, RET_QK_DIM), rk.reshape(bsz, t, RET_HEADS, RET_QK_DIM),
                  rv.reshape(bsz, t, RET_HEADS, RET_V_DIM), positions)
    mu = jnp.mean(y, -1, keepdims=True)
    var = jnp.mean(jnp.square(y - mu), -1, keepdims=True)
    y = ((y - mu) * lax.rsqrt(var + LN_EPS)).reshape(bsz, t, RET_V)
    return jax.nn.silu(rg.astype(F32)) * y


def gated_delta_rule(q, k, v, g, beta):
    bsz, t, h, dk = q.shape
    dv = v.shape[-1]
    c = GDN_CHUNK
    n = t // c
    q = (q * dk ** -0.5).reshape(bsz, n, c, h, dk)
    k = k.reshape(bsz, n, c, h, dk)
    v = v.reshape(bsz, n, c, h, dv)
    beta = beta.reshape(bsz, n, c, h)
    gcum = jnp.cumsum(g.reshape(bsz, n, c, h), axis=2)
    gamma = chunk_decay_matrix(jnp.swapaxes(gcum, 2, 3))
    idx = jnp.arange(c)
    strict = idx[:, None] > idx[None, :]
    kk = jnp.einsum('bnihd,bnjhd->bnhij', k, k)
    m = jnp.where(strict, jnp.swapaxes(beta, 2, 3)[..., :, None] * kk * gamma, 0.0)
    a_mat = m + jnp.eye(c, dtype=F32)
    rhs = jnp.concatenate([v * beta[..., None], k * (beta * jnp.exp(gcum))[..., None]], -1)
    rhs = jnp.swapaxes(rhs, 2, 3)
    sol = lax.linalg.triangular_solve(a_mat, rhs, left_side=True, lower=True, unit_diagonal=True)
    u = sol[..., :dv]
    w = sol[..., dv:]
    qk = jnp.einsum('bnihd,bnjhd->bnhij', q, k) * gamma
    q_dec = jnp.swapaxes(q * jnp.exp(gcum)[..., None], 2, 3)
    k_tail = jnp.swapaxes(k * jnp.exp(gcum[:, :, -1:, :] - gcum)[..., None], 2, 3)
    g_last = jnp.exp(gcum[:, :, -1, :])

    def step(s, xs):
        u_c, w_c, qk_c, qd_c, kt_c, gl_c = xs
        v_new = u_c - jnp.einsum('bhcd,bhde->bhce', w_c, s)
        o = jnp.einsum('bhcd,bhde->bhce', qd_c, s) + jnp.einsum('bhij,bhje->bhie', qk_c, v_new)
        s = s * gl_c[..., None, None] + jnp.einsum('bhcd,bhce->bhde', kt_c, v_new)
        return s, o

    xs = tuple(jnp.moveaxis(a, 1, 0) for a in (u, w, qk, q_dec, k_tail, g_last))
    _, o = lax.scan(step, jnp.zeros((bsz, h, dk, dv), F32), xs)
    o = jnp.swapaxes(jnp.moveaxis(o, 0, 1), 2, 3)
    return o.reshape(bsz, t, h, dv)


def gated_deltanet_mixer(dq, dk_, dv_, dz, da, db, conv_w, a_log, dt_bias, norm_w):
    bsz, t, _ = dq.shape
    qkv = jax.nn.silu(causal_conv(jnp.concatenate([dq, dk_, dv_], -1), conv_w).astype(F32))
    q, k, v = jnp.split(qkv, 3, axis=-1)
    heads = lambda a: a.reshape(bsz, t, GDN_HEADS, GDN_DIM)
    q = l2_norm(heads(q))
    k = l2_norm(heads(k))
    g = -jnp.exp(a_log.astype(F32)) * jax.nn.softplus(da.astype(F32) + dt_bias.astype(F32))
    beta = jax.nn.sigmoid(db.astype(F32))
    o = gated_delta_rule(q, k, heads(v), g, beta)
    o = rms_norm(o, norm_w) * jax.nn.silu(heads(dz).astype(F32))
    return o.reshape(bsz, t, GDN_W)


def s5_mixer(u, lam_re, lam_im, log_step, b_re, b_im, c_re, c_im, d_skip, glu_w, glu_b):
    bsz, t, _ = u.shape
    uf = u.astype(F32)
    ug = uf.reshape(bsz, t, S5_GROUPS, S5_GROUP)
    lre = lam_re.astype(F32)
    lim = lam_im.astype(F32)
    step = jnp.exp(log_step.astype(F32))[:, None]
    mag = jnp.exp(lre * step)
    ab_re = mag * jnp.cos(lim * step)
    ab_im = mag * jnp.sin(lim * step)
    den = lre * lre + lim * lim
    f_re = ((ab_re - 1.0) * lre + ab_im * lim) / den
    f_im = (ab_im * lre - (ab_re - 1.0) * lim) / den
    bb_re = f_re[..., None] * b_re.astype(F32) - f_im[..., None] * b_im.astype(F32)
    bb_im = f_re[..., None] * b_im.astype(F32) + f_im[..., None] * b_re.astype(F32)
    bu_re = jnp.einsum('btgc,gnc->btgn', ug, bb_re)
    bu_im = jnp.einsum('btgc,gnc->btgn', ug, bb_im)
    a_re = jnp.broadcast_to(ab_re, bu_re.shape)
    a_im = jnp.broadcast_to(ab_im, bu_im.shape)

    def combine(e1, e2):
        a1r, a1i, b1r, b1i = e1
        a2r, a2i, b2r, b2i = e2
        return (a2r * a1r - a2i * a1i, a2r * a1i + a2i * a1r,
                a2r * b1r - a2i * b1i + b2r, a2r * b1i + a2i * b1r + b2i)

    _, _, xr, xi = lax.associative_scan(combine, (a_re, a_im, bu_re, bu_im), axis=1)
    y = (jnp.einsum('gcn,btgn->btgc', c_re.astype(F32), xr)
         - jnp.einsum('gcn,btgn->btgc', c_im.astype(F32), xi))
    y = y.reshape(bsz, t, S5_CHANNELS) + d_skip.astype(F32) * uf
    y = jax.nn.gelu(y)
    return y * jax.nn.sigmoid(y @ glu_w.astype(F32) + glu_b.astype(F32))


def ssd_scan(x, dt, a, bm, cm):
    bsz, t, h, p = x.shape
    nst = bm.shape[-1]
    c = SSD_CHUNK
    n = t // c
    xdt = (x * dt[..., None]).reshape(bsz, n, c, h, p)
    acum = jnp.cumsum((dt * a).reshape(bsz, n, c, h), axis=2)
    bm = bm.reshape(bsz, n, c, h, nst)
    cm = cm.reshape(bsz, n, c, h, nst)
    lmat = chunk_decay_matrix(jnp.swapaxes(acum, 2, 3))
    scores = jnp.einsum('bnihs,bnjhs->bnhij', cm, bm) * lmat
    y_diag = jnp.einsum('bnhij,bnjhp->bnihp', scores, xdt)
    decay_states = jnp.exp(acum[:, :, -1:, :] - acum)
    states = jnp.einsum('bnjhs,bnjh,bnjhp->bnhps', bm, decay_states, xdt)
    chunk_decay = jnp.exp(acum[:, :, -1, :])

    def step(s, xs):
        st, dec = xs
        return s * dec[..., None, None] + st, s

    _, prev = lax.scan(step, jnp.zeros((bsz, h, p, nst), F32),
                       (jnp.moveaxis(states, 1, 0), jnp.moveaxis(chunk_decay, 1, 0)))
    prev = jnp.moveaxis(prev, 0, 1)
    y_off = jnp.einsum('bnihs,bnhps,bnih->bnihp', cm, prev, jnp.exp(acum))
    return (y_diag + y_off).reshape(bsz, t, h, p)


def mamba2_mixer(mz, mx, mb, mc, mdt, conv_w, conv_b, dt_bias, a_log, d_skip, norm_w):
    bsz, t, _ = mx.shape
    xbc = causal_conv(jnp.concatenate([mx, mb, mc], -1), conv_w) + conv_b
    xbc = jax.nn.silu(xbc.astype(F32))
    xs, bs, cs = jnp.split(xbc, [SSD_INNER, SSD_INNER + SSD_BC], axis=-1)
    xh = xs.reshape(bsz, t, SSD_HEADS, SSD_HEAD_DIM)
    rep = SSD_HEADS // SSD_GROUPS
    bh = jnp.repeat(bs.reshape(bsz, t, SSD_GROUPS, SSD_STATE), rep, axis=2)
    ch = jnp.repeat(cs.reshape(bsz, t, SSD_GROUPS, SSD_STATE), rep, axis=2)
    dt = jax.nn.softplus(mdt.astype(F32) + dt_bias.astype(F32))
    a = -jnp.exp(a_log.astype(F32))
    y = ssd_scan(xh, dt, a, bh, ch) + d_skip.astype(F32)[:, None] * xh
    y = y.reshape(bsz, t, SSD_INNER) * jax.nn.silu(mz.astype(F32))
    y = rms_norm(y.reshape(bsz, t, SSD_GROUPS, SSD_INNER // SSD_GROUPS),
                 norm_w.reshape(SSD_GROUPS, SSD_INNER // SSD_GROUPS))
    return y.reshape(bsz, t, SSD_INNER)


def setup_inputs(seed: int = 0) -> dict:
    key = jax.random.key(seed)
    keys = list(jax.random.split(key, 40))
    L = DEPTH

    def normal(shape, scale):
        return jax.random.normal(keys.pop(), shape, F32) * scale

    def uniform(shape, lo, hi):
        return jax.random.uniform(keys.pop(), shape, F32, lo, hi)

    def dt_bias(shape):
        dt = jnp.exp(uniform(shape, math.log(DT_MIN), math.log(DT_MAX)))
        return dt + jnp.log(-jnp.expm1(-dt))

    gain = lambda shape: 1.0 + normal(shape, 0.02)
    n_idx = jnp.arange(S5_STATE, dtype=F32)
    return {
        'x': normal((BATCH, SEQ, D_MODEL), 1.0),
        'positions': jnp.broadcast_to(jnp.arange(SEQ, dtype=jnp.int32)[None, :], (BATCH, SEQ)),
        'w_in': normal((L, D_MODEL, D_IN_PROJ), D_MODEL ** -0.5),
        'gdn_conv_w': normal((L, 3 * GDN_W, CONV_WIDTH), CONV_WIDTH ** -0.5),
        'gdn_a_log': jnp.log(uniform((L, GDN_HEADS), 1.0, 16.0)),
        'gdn_dt_bias': dt_bias((L, GDN_HEADS)),
        'gdn_norm_w': gain((L, GDN_DIM)),
        's5_lam_re': -0.5 * (1.0 + normal((L, S5_GROUPS, S5_STATE), 0.01)),
        's5_lam_im': jnp.broadcast_to(math.pi * n_idx, (L, S5_GROUPS, S5_STATE)),
        's5_log_step': uniform((L, S5_GROUPS), math.log(S5_STEP_MIN), math.log(S5_STEP_MAX)),
        's5_b_re': normal((L, S5_GROUPS, S5_STATE, S5_GROUP), (2 * S5_GROUP) ** -0.5),
        's5_b_im': normal((L, S5_GROUPS, S5_STATE, S5_GROUP), (2 * S5_GROUP) ** -0.5),
        's5_c_re': normal((L, S5_GROUPS, S5_GROUP, S5_STATE), S5_STATE ** -0.5),
        's5_c_im': normal((L, S5_GROUPS, S5_GROUP, S5_STATE), S5_STATE ** -0.5),
        's5_d': normal((L, S5_CHANNELS), 1.0),
        's5_glu_w': normal((L, S5_CHANNELS, S5_CHANNELS), S5_CHANNELS ** -0.5),
        's5_glu_b': normal((L, S5_CHANNELS), 0.01),
        'ssd_conv_w': normal((L, SSD_INNER + 2 * SSD_BC, CONV_WIDTH), CONV_WIDTH ** -0.5),
        'ssd_conv_b': normal((L, SSD_INNER + 2 * SSD_BC), 0.01),
        'ssd_dt_bias': dt_bias((L, SSD_HEADS)),
        'ssd_a_log': jnp.log(uniform((L, SSD_HEADS), 1.0, 16.0)),
        'ssd_d': gain((L, SSD_HEADS)),
        'ssd_norm_w': gain((L, SSD_INNER)),
        'w_branch_ret': normal((L, RET_V, D_MODEL), DEEPNORM_BETA * RET_V ** -0.5),
        'w_branch_gdn': normal((L, GDN_W, D_MODEL), DEEPNORM_BETA * GDN_W ** -0.5),
        'w_branch_s5': normal((L, S5_CHANNELS, D_MODEL), DEEPNORM_BETA * S5_CHANNELS ** -0.5),
        'w_branch_ssd': normal((L, SSD_INNER, D_MODEL), DEEPNORM_BETA * SSD_INNER ** -0.5),
        'w_out': normal((L, D_MODEL, D_MODEL), DEEPNORM_BETA * D_MODEL ** -0.5),
        'ln1_g': gain((L, D_MODEL)),
        'ln1_b': normal((L, D_MODEL), 0.02),
        'w_up': normal((L, D_MODEL, D_FF), DEEPNORM_BETA * D_MODEL ** -0.5),
        'w_down': normal((L, D_FF, D_MODEL), DEEPNORM_BETA * D_FF ** -0.5),
        'ln2_g': gain((L, D_MODEL)),
        'ln2_b': normal((L, D_MODEL), 0.02),
    }


def reference(x, positions, w_in, gdn_conv_w, gdn_a_log, gdn_dt_bias, gdn_norm_w,
              s5_lam_re, s5_lam_im, s5_log_step, s5_b_re, s5_b_im, s5_c_re, s5_c_im,
              s5_d, s5_glu_w, s5_glu_b, ssd_conv_w, ssd_conv_b, ssd_dt_bias, ssd_a_log,
              ssd_d, ssd_norm_w, w_branch_ret, w_branch_gdn, w_branch_s5, w_branch_ssd,
              w_out, ln1_g, ln1_b, w_up, w_down, ln2_g, ln2_b):
    bsz, t, _ = x.shape
    dt_ = x.dtype
    for l in range(DEPTH):
        proj = x @ w_in[l]
        (r_q, r_k, r_v, r_g, d_q, d_k, d_v, d_z, d_a, d_b, s_u,
         m_z, m_x, m_b, m_c, m_dt, gate_logits) = jnp.split(proj, SPLIT_POINTS, axis=-1)
        y_ret = retention_mixer(r_q, r_k, r_v, r_g, positions)
        y_gdn = gated_deltanet_mixer(d_q, d_k, d_v, d_z, d_a, d_b, gdn_conv_w[l], gdn_a_log[l],
                                     gdn_dt_bias[l], gdn_norm_w[l])
        y_s5 = s5_mixer(s_u, s5_lam_re[l], s5_lam_im[l], s5_log_step[l], s5_b_re[l], s5_b_im[l],
                        s5_c_re[l], s5_c_im[l], s5_d[l], s5_glu_w[l], s5_glu_b[l])
        y_ssd = mamba2_mixer(m_z, m_x, m_b, m_c, m_dt, ssd_conv_w[l], ssd_conv_b[l], ssd_dt_bias[l],
                             ssd_a_log[l], ssd_d[l], ssd_norm_w[l])
        gates = jax.nn.sigmoid(gate_logits.reshape(bsz, t, N_BRANCH, D_MODEL))
        merged = (gates[:, :, 0] * (y_ret.astype(dt_) @ w_branch_ret[l])
                  + gates[:, :, 1] * (y_gdn.astype(dt_) @ w_branch_gdn[l])
                  + gates[:, :, 2] * (y_s5.astype(dt_) @ w_branch_s5[l])
                  + gates[:, :, 3] * (y_ssd.astype(dt_) @ w_branch_ssd[l]))
        mix = merged @ w_out[l]
        x = layer_norm(DEEPNORM_ALPHA * x + mix, ln1_g[l], ln1_b[l])
        hid = jnp.square(jax.nn.relu(x @ w_up[l])) @ w_down[l]
        x = layer_norm(DEEPNORM_ALPHA * x + hid, ln2_g[l], ln2_b[l])
    return x
```

```python
import contextlib
import numpy as np
import concourse.bass as bass
import concourse.mybir as mybir
from concourse.bass_utils import run_bass_kernel_spmd

F32 = mybir.dt.float32
BF16 = mybir.dt.bfloat16
AF = mybir.ActivationFunctionType
ALU = mybir.AluOpType
AX = mybir.AxisListType

NCORES = 8
D = 2048
T = 8192
DEPTH = 2
NT = T // NCORES
DFF = 8192
ALPHA = (2 * DEPTH) ** 0.25
LN_EPS = 1e-5
RMS_EPS = 1e-6

ENGS = ("pe", "act", "dve", "pool", "sp")


class Prog:
    def __init__(self, nc, n_dma_sems=12):
        self.nc = nc
        self.ops = []
        self.streams = {e: [] for e in ENGS}
        self.last_w = {}
        self.readers = {}
        self.nds = n_dma_sems
        self.dma_cnt = {e: [0] * n_dma_sems for e in ENGS + ("poolg",)}
        self.dma_rr = {e: 0 for e in ENGS + ("poolg",)}

    def _track(self, oid, reads, writes):
        deps = set()
        for k in reads:
            if k in self.last_w:
                deps.add(self.last_w[k])
        for k in writes:
            if k in self.last_w:
                deps.add(self.last_w[k])
            deps.update(self.readers.get(k, ()))
        for k in reads:
            self.readers.setdefault(k, []).append(oid)
        for k in writes:
            self.last_w[k] = oid
            self.readers[k] = []
        deps.discard(oid)
        return deps

    def op(self, eng, fn, reads=(), writes=()):
        oid = len(self.ops)
        deps = self._track(oid, reads, writes)
        self.ops.append(dict(eng=eng, fn=fn, deps=deps, dma=None, ms=None))
        self.streams[eng].append(oid)
        return oid

    def dma(self, eng, out, in_, reads=(), writes=(), **kw):
        oid = len(self.ops)
        deps = self._track(oid, reads, writes)
        si = self.dma_rr[eng]
        self.dma_rr[eng] = (si + 1) % self.nds
        self.dma_cnt[eng][si] += 16
        val = self.dma_cnt[eng][si]

        def fn(e, out=out, in_=in_, kw=kw):
            return e.dma_start(out=out, in_=in_, **kw)

        self.ops.append(dict(eng=eng, fn=fn, deps=deps, dma=(eng, si, val), ms=None))
        self.streams[eng].append(oid)
        return oid

    def cc(self, kind, ins, outs, reads=(), writes=()):
        oid = len(self.ops)
        deps = self._track(oid, reads, writes)
        idx = getattr(self, "n_cc", 0)
        self.n_cc = idx + 1

        def fn(e):
            return e.collective_compute(kind, ALU.bypass, replica_groups=[list(range(NCORES))], ins=ins, outs=outs)

        self.ops.append(dict(eng="pool", fn=fn, deps=deps, dma=("cc", idx), ms=None))
        self.streams["pool"].append(oid)
        self.barrier()
        return oid

    def gather(self, out, in2d, idx_ap, nrows, reads=(), writes=()):
        oid = len(self.ops)
        deps = self._track(oid, reads, writes)
        eng = "pool"
        GG = getattr(self, "gather_grp", "poolg")
        si = self.dma_rr[GG]
        self.dma_rr[GG] = (si + 1) % self.nds
        self.dma_cnt[GG][si] += 16
        val = self.dma_cnt[GG][si]

        def fn(e):
            if getattr(self, "_bcreg", None) is None:
                self._bcreg = e.to_reg(nrows - 1)
            return e.indirect_dma_start(out=out, out_offset=None, in_=in2d,
                                        in_offset=bass.IndirectOffsetOnAxis(ap=idx_ap, axis=0),
                                        bounds_check=self._bcreg, oob_is_err=False)

        self.ops.append(dict(eng=eng, fn=fn, deps=deps, dma=(GG, si, val), ms=None))
        self.streams[eng].append(oid)
        return oid

    def barrier(self):
        deps = set()
        for e in ENGS:
            for oid in reversed(self.streams[e]):
                if not self.ops[oid].get("noop"):
                    deps.add(oid)
                    break
        deps.update(i for i in range(getattr(self, "_bar_from", 0), len(self.ops)) if self.ops[i]["dma"] is not None)
        self._bar_from = len(self.ops)
        for e in ENGS:
            oid = len(self.ops)
            self.ops.append(dict(eng=e, fn=lambda h: None, deps=set(deps), dma=None, ms=None, noop=True))
            self.streams[e].append(oid)

    def emit(self):
        nc = self.nc
        ops = self.ops
        need = set()
        for o in ops:
            for d in o["deps"]:
                dd = ops[d]
                if dd["dma"] is None and not (dd["eng"] == "pe" and o["eng"] == "pe"):
                    need.add(d)
        cnt = {e: 0 for e in ENGS}
        for e in ENGS:
            for oid in self.streams[e]:
                if oid in need:
                    cnt[e] += 1
                    ops[oid]["ms"] = cnt[e]
        with contextlib.ExitStack() as st:
            esem = {e: st.enter_context(nc.semaphore("es_" + e)) for e in ENGS}
            dsem = {}
            for e in ENGS + ("poolg",):
                if any(c > 0 for c in self.dma_cnt[e]):
                    dsem[e] = [st.enter_context(nc.semaphore("ds_%s_%d" % (e, i))) for i in range(self.nds)]
            ccsem = [st.enter_context(nc.semaphore("cc_%d" % i)) for i in range(getattr(self, "n_cc", 0))]
            block = st.enter_context(nc.Block())

            def run_stream(e, h):
                waited = {}

                def wait(key, sem, val):
                    if val <= 0 or waited.get(key, 0) >= val:
                        return
                    h.wait_ge(sem, val)
                    waited[key] = val

                for oid in self.streams[e]:
                    o = ops[oid]
                    for d in sorted(o["deps"]):
                        dd = ops[d]
                        if dd["dma"] is not None and dd["dma"][0] == "cc":
                            wait(("cc", dd["dma"][1]), ccsem[dd["dma"][1]], 1)
                        elif dd["dma"] is not None:
                            grp, si, val = dd["dma"]
                            wait(("d", grp, si), dsem[grp][si], val)
                        else:
                            if dd["eng"] == "pe" and e == "pe":
                                continue
                            wait(("e", dd["eng"]), esem[dd["eng"]], dd["ms"])
                    if o["dma"] is not None and o["dma"][0] != "cc":
                        grp, si, val = o["dma"]
                        wait(("d", grp, si), dsem[grp][si], val - 16)
                    ins = o["fn"](h)
                    if ins is None:
                        continue
                    if o["dma"] is not None and o["dma"][0] == "cc":
                        ins.then_inc(ccsem[o["dma"][1]])
                    elif o["dma"] is not None:
                        ins.then_inc(dsem[o["dma"][0]][o["dma"][1]], 16)
                    elif o["ms"] is not None:
                        ins.then_inc(esem[e], 1)

            @block.tensor
            def _(h):
                run_stream("pe", h)

            @block.scalar
            def _(h):
                run_stream("act", h)

            @block.vector
            def _(h):
                run_stream("dve", h)

            @block.gpsimd
            def _(h):
                run_stream("pool", h)

            @block.sync
            def _(h):
                run_stream("sp", h)


class Ctx:
    def __init__(self, nc):
        self.nc = nc
        self.P = Prog(nc)
        self.st = contextlib.ExitStack()
        self.ps = [self.st.enter_context(nc.psum_tensor("ps%d" % i, [128, 512], F32)) for i in range(8)]
        self.ps_rr = 0
        self.uid = 0

    def sb(self, name, shape, dt, scope=None):
        self.uid += 1
        return (scope or self.st).enter_context(self.nc.sbuf_tensor("%s_%d" % (name, self.uid), shape, dt))

    def psum(self):
        while True:
            i = self.ps_rr
            self.ps_rr = (i + 1) % 8
            if i not in getattr(self, "reserved", ()):
                return i

    def psum_reserve(self):
        i = self.psum()
        self.reserved = set(getattr(self, "reserved", ())) | {i}
        return i

    def psum_release(self, i):
        self.reserved.discard(i)

    def mm(self, psi, out, lhsT, rhs, start, stop, reads):
        self.P.op("pe", lambda h: h.matmul(out, lhsT, rhs, start=start, stop=stop),
                  reads=reads, writes=[("ps", psi)])

    def act(self, out, in_, func, reads, writes, bias=None, scale=None):
        kw = {}
        if bias is not None:
            kw["bias"] = bias
        if scale is not None:
            kw["scale"] = scale
        self.P.op("act", lambda h: h.activation(out, in_, func, **kw), reads=reads, writes=writes)

    def tt(self, out, in0, in1, op, reads, writes, eng="dve"):
        self.P.op(eng, lambda h: h.tensor_tensor(out, in0, in1, op), reads=reads, writes=writes)

    def ts(self, out, in0, s1, s2, op0, op1, reads, writes, eng="dve"):
        if op1 is None:
            self.P.op(eng, lambda h: h.tensor_scalar(out, in0, s1, None, op0), reads=reads, writes=writes)
        else:
            self.P.op(eng, lambda h: h.tensor_scalar(out, in0, s1, s2, op0, op1), reads=reads, writes=writes)

    def stt(self, out, in0, scalar, in1, op0, op1, reads, writes):
        self.P.op("dve", lambda h: h.scalar_tensor_tensor(out, in0, scalar, in1, op0, op1),
                  reads=reads, writes=writes)

    def copy(self, out, in_, reads, writes, eng="dve"):
        if eng == "act":
            self.P.op(eng, lambda h: h.activation(out, in_, AF.Copy), reads=reads, writes=writes)
        else:
            self.P.op(eng, lambda h: h.tensor_copy(out, in_), reads=reads, writes=writes)

    def memset(self, ap, val, writes, eng="dve"):
        self.P.op(eng, lambda h: h.memset(ap, val), writes=writes)

    def recip(self, out, in_, reads, writes):
        self.P.op("dve", lambda h: h.reciprocal(out, in_), reads=reads, writes=writes)

    def load(self, out, in_, writes, eng="sp", reads=()):
        self.P.dma(eng, out, in_, reads=reads, writes=writes)

    def finish(self, keys):
        self.P.op("sp", lambda h: None, reads=keys)
        self.P.emit()
        self.st.close()


def dense_layer(C, stg, E, nt, tag):
    TB = 512
    NB = nt // TB
    P = C.P
    ps = C.ps
    xT, x1s = E["xT"], E["x1s"]
    ones = C.sb("ones", [128, 128], F32, stg)
    C.memset(ones[:], 1.0, [("ones",)])
    vt = C.sb("vecs_sb", [128, 80], F32, stg)
    C.load(vt[:], E["vecs"], [("vecs",)])
    GLU_B, SSD_NW, LN1G, LN1B, LN2G, LN2B = 0, 8, 16, 32, 48, 64
    NST = 2
    wst = C.sb("wst", [128, NST, 16 * 512], BF16, stg)
    wst_rr = [0]

    def wload(src2d, k0, nk, c0, ncols):
        s = wst_rr[0]
        wst_rr[0] = (s + 1) % NST
        view = wst[:, s, 0:nk * ncols].rearrange("p (k c) -> p k c", k=nk)
        src = src2d[k0 * 128:(k0 + nk) * 128, c0:c0 + ncols].rearrange("(k p) c -> p k c", p=128)
        C.load(view, src, [("wst", s)], eng="pool")
        return view, ("wst", s)

    NSC = 6
    sc = C.sb("sc", [128, NSC, TB], F32, stg)
    sc_rr = [0]

    def scratch():
        i = sc_rr[0]
        sc_rr[0] = (i + 1) % NSC
        return sc[:, i, :], ("sc", i)

    mT = C.sb("mT", [128, 16, TB], BF16, stg)
    x1b = C.sb("x1b", [128, 16, TB], BF16, stg)

    def rstd_inplace(v, vk):
        C.act(v, v, AF.Sqrt, [vk], [vk])
        C.recip(v, v, [vk], [vk])

    def layer_norm_T(z, tb, gcol, bcol, dst_dram, dtag, dst_bf16, bf_dram):
        p1 = C.psum()
        p2 = C.psum()
        for f in range(16):
            C.mm(p1, ps[p1][:], ones[:], z[:, f, :], f == 0, f == 15, [("z", f), ("ones",)])
        for f in range(16):
            s_, sk = scratch()
            C.act(s_, z[:, f, :], AF.Square, [("z", f)], [sk])
            C.mm(p2, ps[p2][:], ones[:], s_, f == 0, f == 15, [sk, ("ones",)])
        mean, mk = scratch()
        C.ts(mean, ps[p1][:], 1.0 / D, None, ALU.mult, None, [("ps", p1)], [mk])
        var, vk = scratch()
        C.tt(var, mean, mean, ALU.mult, [mk], [vk])
        C.stt(var, ps[p2][:], 1.0 / D, var, ALU.mult, ALU.subtract, [("ps", p2), vk], [vk])
        C.ts(var, var, LN_EPS, None, ALU.add, None, [vk], [vk])
        rstd_inplace(var, vk)
        for f in range(16):
            zf = z[:, f, :]
            zk = ("z", f)
            C.tt(zf, zf, mean, ALU.subtract, [zk, mk], [zk])
            C.tt(zf, zf, var, ALU.mult, [zk, vk], [zk])
            C.ts(zf, zf, vt[:, gcol + f:gcol + f + 1], vt[:, bcol + f:bcol + f + 1], ALU.mult, ALU.add,
                 [zk, ("vecs",)], [zk])
            if dst_bf16 is not None:
                C.copy(dst_bf16[:, f, :], zf, [zk], [("x1b", f)], eng="act")
                if bf_dram is not None:
                    C.load(bf_dram[f * 128:(f + 1) * 128, tb * TB:(tb + 1) * TB], dst_bf16[:, f, :],
                           [("dram", tag, dtag + "b", f, tb)], reads=[("x1b", f)])
            C.load(dst_dram[f * 128:(f + 1) * 128, tb * TB:(tb + 1) * TB], zf, [("dram", tag, dtag, f, tb)], reads=[zk])

    for tb in range(NB):
        tsl = slice(tb * TB, (tb + 1) * TB)
        with contextlib.ExitStack() as sa:
            xTb = C.sb("xTb", [128, 16, TB], BF16, sa)
            if E.get("xb") is not None:
                C.load(xTb[:], E["xb"][:, tsl].rearrange("(k p) t -> p k t", p=128), [("xTb", kc) for kc in range(16)])
            else:
                for kc in range(16):
                    C.load(xTb[:, kc, :], xT[kc * 128:(kc + 1) * 128, tsl], [("xTb", kc)], eng="pool")
            yT = [C.sb("yT%d" % b, [128, 8, TB], BF16, sa) for b in range(4)]
            gel = C.sb("gel", [128, 8, TB], F32, sa)
            gelb = C.sb("gelb", [128, 8, TB], BF16, sa)
            acc = C.sb("acc", [128, 4, TB], F32, sa)
            load_y = E["load_y"]
            for ct in range(8):
                a, ak = scratch()
                load_y(1, ct, tb, a, ak)
                C.copy(yT[1][:, ct, :], a, [ak], [("yT", 1, ct)], eng="act")
            for g2 in range(2):
                wv, wk = wload(E["w_rg"], 0, 16, g2 * 512, 512)
                for hh in range(2):
                    h = g2 * 2 + hh
                    yt = []
                    for j in range(2):
                        ct = 2 * h + j
                        a, ak = scratch()
                        load_y(0, ct, tb, a, ak)
                        yt.append((a, ak, ct))
                    p1 = C.psum()
                    p2 = C.psum()
                    for j, (a, ak, ct) in enumerate(yt):
                        C.mm(p1, ps[p1][:], ones[:], a, j == 0, j == 1, [ak, ("ones",)])
                    for j, (a, ak, ct) in enumerate(yt):
                        s_, sk = scratch()
                        C.act(s_, a, AF.Square, [ak], [sk])
                        C.mm(p2, ps[p2][:], ones[:], s_, j == 0, j == 1, [sk, ("ones",)])
                    mean, mk = scratch()
                    C.ts(mean, ps[p1][:], 1.0 / 256, None, ALU.mult, None, [("ps", p1)], [mk])
                    var, vk = scratch()
                    C.tt(var, mean, mean, ALU.mult, [mk], [vk])
                    C.stt(var, ps[p2][:], 1.0 / 256, var, ALU.mult, ALU.subtract, [("ps", p2), vk], [vk])
                    C.ts(var, var, LN_EPS, None, ALU.add, None, [vk], [vk])
                    rstd_inplace(var, vk)
                    for j, (a, ak, ct) in enumerate(yt):
                        C.tt(a, a, mean, ALU.subtract, [ak, mk], [ak])
                        C.tt(a, a, var, ALU.mult, [ak, vk], [ak])
                        pg = C.psum()
                        cc = (ct % 4) * 128
                        for kc in range(16):
                            C.mm(pg, ps[pg][:], wv[:, kc, cc:cc + 128], xTb[:, kc, :], kc == 0, kc == 15,
                                 [wk, ("xTb", kc)])
                        C.act(acc[:, j, :], ps[pg][:], AF.Silu, [("ps", pg)], [("acc", j)])
                        C.tt(yT[0][:, ct, :], a, acc[:, j, :], ALU.mult, [ak, ("acc", j)], [("yT", 0, ct)])
            for g in range(2):
                tl = []
                p2 = C.psum()
                for j in range(4):
                    ct = 4 * g + j
                    a = gel[:, j, :]
                    ak = ("gel", j)
                    load_y(3, ct, tb, a, ak)
                    tl.append((a, ak, ct))
                for j, (a, ak, ct) in enumerate(tl):
                    s_, sk = scratch()
                    C.act(s_, a, AF.Square, [ak], [sk])
                    C.mm(p2, ps[p2][:], ones[:], s_, j == 0, j == 3, [sk, ("ones",)])
                r_, rk = scratch()
                C.ts(r_, ps[p2][:], 1.0 / 512, RMS_EPS, ALU.mult, ALU.add, [("ps", p2)], [rk])
                rstd_inplace(r_, rk)
                for j, (a, ak, ct) in enumerate(tl):
                    C.stt(yT[3][:, ct, :], a, vt[:, SSD_NW + ct:SSD_NW + ct + 1], r_, ALU.mult, ALU.mult,
                          [ak, rk, ("vecs",)], [("yT", 3, ct)])
            for ct in range(8):
                a, ak = scratch()
                load_y(2, ct, tb, a, ak)
                C.act(gel[:, ct, :], a, AF.Gelu, [ak], [("gel", ct)])
                C.copy(gelb[:, ct, :], gel[:, ct, :], [("gel", ct)], [("gelb", ct)])
            for g2 in range(2):
                wv, wk = wload(E["glu_w"], 0, 8, g2 * 512, 512)
                for oi in range(4):
                    ot = g2 * 4 + oi
                    pg = C.psum()
                    for kc in range(8):
                        C.mm(pg, ps[pg][:], wv[:, kc, oi * 128:(oi + 1) * 128], gelb[:, kc, :], kc == 0, kc == 7,
                             [wk, ("gelb", kc)])
                    s_, sk = scratch()
                    C.act(s_, ps[pg][:], AF.Sigmoid, [("ps", pg), ("vecs",)], [sk], bias=vt[:, GLU_B + ot:GLU_B + ot + 1])
                    C.tt(yT[2][:, ot, :], s_, gel[:, ot, :], ALU.mult, [sk, ("gel", ot)], [("yT", 2, ot)])
            for fg in range(4):
                for b in range(4):
                    gv, gk = wload(E["w_gate"], 0, 16, b * D + fg * 512, 512)
                    bv, bk = wload(E["wb"][b], 0, 8, fg * 512, 512)
                    for fi in range(4):
                        f = fg * 4 + fi
                        pa = C.psum()
                        pb = C.psum()
                        for kc in range(16):
                            C.mm(pa, ps[pa][:], gv[:, kc, fi * 128:(fi + 1) * 128], xTb[:, kc, :], kc == 0, kc == 15,
                                 [gk, ("xTb", kc)])
                        for kc in range(8):
                            C.mm(pb, ps[pb][:], bv[:, kc, fi * 128:(fi + 1) * 128], yT[b][:, kc, :], kc == 0, kc == 7,
                                 [bk, ("yT", b, kc)])
                        s_, sk = scratch()
                        C.act(s_, ps[pa][:], AF.Sigmoid, [("ps", pa)], [sk])
                        ak = ("acc", fi)
                        if b == 0:
                            C.tt(acc[:, fi, :], s_, ps[pb][:], ALU.mult, [sk, ("ps", pb)], [ak])
                        else:
                            C.tt(s_, s_, ps[pb][:], ALU.mult, [sk, ("ps", pb)], [sk])
                            if b < 3:
                                C.tt(acc[:, fi, :], acc[:, fi, :], s_, ALU.add, [sk, ak], [ak])
                            else:
                                C.tt(mT[:, f, :], acc[:, fi, :], s_, ALU.add, [sk, ak], [("mT", f)])
            P.barrier()
        with contextlib.ExitStack() as sb_:
            zbuf = C.sb("zbuf", [128, 16, TB], F32, sb_)
            for fg in range(4):
                wv, wk = wload(E["w_out"], 0, 16, fg * 512, 512)
                for fi in range(4):
                    f = fg * 4 + fi
                    pa = C.psum()
                    for kc in range(16):
                        C.mm(pa, ps[pa][:], wv[:, kc, fi * 128:(fi + 1) * 128], mT[:, kc, :], kc == 0, kc == 15,
                             [wk, ("mT", kc)])
                    a, ak = scratch()
                    C.load(a, xT[f * 128:(f + 1) * 128, tsl], [ak])
                    C.stt(zbuf[:, f, :], a, ALPHA, ps[pa][:], ALU.mult, ALU.add, [ak, ("ps", pa)], [("z", f)])
            layer_norm_T(zbuf, tb, LN1G, LN1B, x1s, "x1", x1b, None)
            P.barrier()
        with contextlib.ExitStack() as sc_:
            zbuf = C.sb("zbuf", [128, 16, TB], F32, sc_)
            hT = C.sb("hT", [128, 64, TB], BF16, sc_)
            for cg in range(16):
                wv, wk = wload(E["w_up"], 0, 16, cg * 512, 512)
                for ci in range(4):
                    fc = cg * 4 + ci
                    pa = C.psum()
                    for kc in range(16):
                        C.mm(pa, ps[pa][:], wv[:, kc, ci * 128:(ci + 1) * 128], x1b[:, kc, :], kc == 0, kc == 15,
                             [wk, ("x1b", kc)])
                    a, ak = scratch()
                    C.act(a, ps[pa][:], AF.Relu, [("ps", pa)], [ak])
                    C.tt(hT[:, fc, :], a, a, ALU.mult, [ak], [("hT", fc)])
            for fg in range(4):
                pas = [C.psum() for _ in range(4)]
                for kq in range(4):
                    wv, wk = wload(E["w_down"], kq * 16, 16, fg * 512, 512)
                    for fi in range(4):
                        for k in range(16):
                            fc = kq * 16 + k
                            C.mm(pas[fi], ps[pas[fi]][:], wv[:, k, fi * 128:(fi + 1) * 128], hT[:, fc, :], fc == 0, fc == 63,
                                 [wk, ("hT", fc)])
                for fi in range(4):
                    f = fg * 4 + fi
                    a, ak = scratch()
                    C.load(a, x1s[f * 128:(f + 1) * 128, tsl], [ak], reads=[("dram", tag, "x1", f, tb)])
                    C.stt(zbuf[:, f, :], a, ALPHA, ps[pas[fi]][:], ALU.mult, ALU.add, [ak, ("ps", pas[fi])], [("z", f)])
            layer_norm_T(zbuf, tb, LN2G, LN2B, E["out"], "out", x1b if E.get("out_bf") is not None else None, E.get("out_bf"))
            P.barrier()


def build_dense(nt=NT, phases=()):
    nc = bass.Bass("TRN2", target_bir_lowering=False)

    def din(name, shape):
        return nc.dram_tensor(name, shape, F32, kind="ExternalInput").ap()

    E = {"xT": din("xT", [D, nt])}
    ysrc = [din(n, [1024, nt]) for n in ("retT", "gdnT", "s5T", "ssdT")]
    E["w_gate"] = din("w_gate", [D, 4 * D])
    E["w_rg"] = din("w_rg", [D, 1024])
    E["wb"] = [din("wb%d" % b, [1024, D]) for b in range(4)]
    E["w_out"] = din("w_out", [D, D])
    E["w_up"] = din("w_up", [D, DFF])
    E["w_down"] = din("w_down", [DFF, D])
    E["glu_w"] = din("glu_w", [1024, 1024])
    E["vecs"] = din("vecs", [128, 80])
    E["out"] = nc.dram_tensor("outT", [D, nt], F32, kind="ExternalOutput").ap()
    E["x1s"] = nc.dram_tensor("x1s", [D, nt], F32, kind="Internal").ap()
    C = Ctx(nc)

    def load_y(m, ct, tb, dst, key):
        C.load(dst, ysrc[m][ct * 128:(ct + 1) * 128, tb * 512:(tb + 1) * 512], [key])

    E["load_y"] = load_y
    dense_layer(C, C.st, E, nt, "d")
    C.finish([k for k in list(C.P.last_w.keys()) if k[0] == "dram"])
    return nc


def dense_inputs(l, c, inp, xT_full, retT, gdnT, s5T, ssdT, nt=NT):
    sl = slice(c * nt, (c + 1) * nt)
    w_in = inp["w_in"][l]
    colv = lambda v, n: np.ascontiguousarray(np.asarray(v, np.float32).reshape(n, 128).T)
    vecs = np.concatenate([colv(inp["s5_glu_b"][l], 8), colv(inp["ssd_norm_w"][l], 8), colv(inp["ln1_g"][l], 16),
                           colv(inp["ln1_b"][l], 16), colv(inp["ln2_g"][l], 16), colv(inp["ln2_b"][l], 16)], axis=1)
    act = {} if xT_full is None else {
        "xT": np.ascontiguousarray(xT_full[:, sl]),
        "retT": np.ascontiguousarray(retT[:, sl]), "gdnT": np.ascontiguousarray(gdnT[:, sl]),
        "s5T": np.ascontiguousarray(s5T[:, sl]), "ssdT": np.ascontiguousarray(ssdT[:, sl])}
    return {
        **act,
        "w_gate": np.ascontiguousarray(w_in[:, 10784:]), "w_rg": np.ascontiguousarray(w_in[:, 2048:3072]),
        "wb0": inp["w_branch_ret"][l], "wb1": inp["w_branch_gdn"][l], "wb2": inp["w_branch_s5"][l],
        "wb3": inp["w_branch_ssd"][l], "w_out": inp["w_out"][l], "w_up": inp["w_up"][l], "w_down": inp["w_down"][l],
        "glu_w": inp["s5_glu_w"][l], "vecs": np.ascontiguousarray(vecs),
    }


NEG = -30000.0
_CO = {}
_off = 0
for _n, _w in [("ident", 128), ("ones", 128), ("mb_incl", 128), ("strict", 128), ("E0", 128), ("E1", 128), ("E32", 128),
               ("E64", 128), ("El127", 128), ("El63", 128), ("shift64", 128), ("cmask", 512), ("ret_dmat", 128),
               ("ret_qdec", 128), ("ret_kdec", 1), ("ret_gc", 1), ("rope_inv", 1), ("rope_sign", 1),
               ("nsel0", 1), ("nsel32", 1), ("nsel64", 1), ("sgn_pm", 1), ("sgn_mm", 1), ("tau1", 512)]:
    _CO[_n] = (_off, _w)
    _off += _w
NCONST = _off


def make_consts(c):
    k = np.zeros((128, NCONST), np.float64)

    def put(name, arr):
        o, w = _CO[name]
        k[:, o:o + w] = np.asarray(arr, np.float64).reshape(128, w)

    i = np.arange(128)
    put("ident", np.eye(128))
    put("ones", np.ones((128, 128)))
    put("mb_incl", np.where(i[None, :] >= i[:, None], 0.0, NEG))
    put("strict", np.where(i[None, :] > i[:, None], 1.0, 0.0))
    for r in (0, 1, 32, 64):
        e = np.zeros((128, 128)); e[r, :] = 1.0
        put("E%d" % r, e)
    e = np.zeros((128, 128)); e[127, :] = 1.0; put("El127", e)
    e = np.zeros((128, 128)); e[63, :] = 1.0; put("El63", e)
    sh = np.zeros((128, 128)); sh[i, (i + 64) % 128] = 1.0; put("shift64", sh)
    cm = np.ones((128, 512)); cm[:, 0::128] = 0.0; cm[0:32, 0::64] = 0.0; put("cmask", cm)
    h = c // 2
    lg = np.log(1.0 - 2.0 ** (-5.0 - h))
    s = 128.0 ** -0.5
    d = i[None, :] - i[:, None]
    put("ret_dmat", np.where(d >= 0, np.exp(np.maximum(d, 0) * lg) * s, 0.0))
    put("ret_qdec", np.broadcast_to(np.exp((i + 1.0) * lg)[None, :], (128, 128)))
    put("ret_kdec", np.exp((127.0 - i) * lg) * s)
    put("ret_gc", np.full(128, np.exp(128.0 * lg)))
    put("rope_inv", (10000.0 ** (-(np.arange(64, dtype=np.float32) / np.float32(64))).astype(np.float64))[i % 64])
    put("rope_sign", np.where(i < 64, -1.0, 1.0))
    for r in (0, 32, 64):
        v = np.zeros(128); v[r] = -1.0; put("nsel%d" % r, v)
    put("sgn_pm", np.where(i < 64, 1.0, -1.0))
    put("sgn_mm", np.full(128, -1.0))
    put("tau1", np.broadcast_to(np.arange(1, 513, dtype=np.float64)[None, :], (128, 512)))
    return k.astype(np.float32)


TWO_PI = 2.0 * np.pi
CW1 = float(np.float32(6.28125))
CW2 = float(np.float32(TWO_PI - 6.28125))


class Mix:
    def __init__(self, nt, phases, standalone=True):
        self.nt = nt
        self.TB = 512
        self.NBLK = nt // self.TB
        self.ybuf = None
        self.xg = None
        if not standalone:
            return
        nc = self.nc = bass.Bass("TRN2", target_bir_lowering=False)
        din = lambda name, shape, dt=F32: nc.dram_tensor(name, shape, dt, kind="ExternalInput").ap()
        self.xT = din("xT", [D, nt])
        self.pos = din("pos", [1, nt], mybir.dt.int32)
        self.wmix = din("wmix", [D, 15 * 128])
        self.consts_d = din("consts", [128, NCONST])
        self.prm_d = din("prm", [128, 64])
        self.nw_d = din("gdn_nw", [128, 128])
        self.s5m_d = din("s5m", [128, 4, 8, 128])
        dout = lambda name, shape: nc.dram_tensor(name, shape, F32, kind="ExternalOutput").ap()
        self.outs = [dout("o_ret", [128, nt]), dout("o_gdn", [128, nt]), dout("o_s5", [128, nt]), dout("o_ssd", [128, nt])]
        C = Ctx(nc)
        self.run(C, C.st, phases, "m")
        C.finish([k for k in list(C.P.last_w.keys()) if k[0] == "dram"])

    def run(self, C, stg, phases, tag):
        self.C = C
        self.tag = tag
        self.ps = C.ps
        self.K = C.sb("consts", [128, NCONST], F32, stg)
        C.load(self.K[:], self.consts_d, [("K",)])
        self.prm = C.sb("prm", [128, 64], F32, stg)
        C.load(self.prm[:], self.prm_d, [("prm",)])
        self.xb = C.sb("xb", [128, 16, self.TB], BF16, stg)
        self.sc = C.sb("msc", [128, 10, self.TB], F32, stg)
        self.sc_rr = 0
        self.ss = C.sb("mss", [128, 16], F32, stg)
        self.ss_rr = 0
        if "ssd" in phases:
            self.phase_ssd()
        if "ret" in phases:
            self.phase_ret()
        if "gdn" in phases:
            self.phase_gdn()
        if "s5" in phases:
            self.phase_s5()

    def out_ap(self, m, blk, c0, w):
        if self.ybuf is not None:
            return self.ybuf[m, :, blk, c0:c0 + w]
        t = blk * self.TB + c0
        return self.outs[m][:, t:t + w]

    def kc(self, name, rows=128):
        o, w = _CO[name]
        return self.K[0:rows, o:o + w]

    def scratch(self, w=None):
        i = self.sc_rr
        self.sc_rr = (i + 1) % 10
        w = w or self.TB
        return self.sc[:, i, 0:w], ("msc", i)

    def small(self):
        i = self.ss_rr
        self.ss_rr = (i + 1) % 16
        return self.ss[:, i:i + 1], ("mss", i)

    def load_x(self, blk):
        C = self.C
        t0 = blk * self.TB
        if self.xg is not None:
            r, off = blk // 2, (blk % 2) * self.TB
            src = self.xg[r * D:(r + 1) * D, off:off + self.TB].rearrange("(k p) t -> p k t", p=128)
            C.load(self.xb[:], src, [("xb",)], reads=[("xg", self.tag)])
            return
        src = self.xT[:, t0:t0 + self.TB].rearrange("(k p) t -> p k t", p=128)
        C.load(self.xb[:], src, [("xb",)], eng="pool")

    def load_w(self, wt, tiles):
        C = self.C
        for j, ti in enumerate(tiles):
            src = self.wmix[:, ti * 128:(ti + 1) * 128].rearrange("(k p) c -> p k c", p=128)
            C.load(wt[:, :, j * 128:(j + 1) * 128], src, [("wt", j)], eng="pool")

    def proj(self, wt, j):
        C = self.C
        p = C.psum()
        for kc in range(16):
            C.mm(p, self.ps[p][:], wt[:, kc, j * 128:(j + 1) * 128], self.xb[:, kc, :], kc == 0, kc == 15,
                 [("wt", j), ("xb",)])
        return p

    def tr(self, in_ap, in_keys, rows, cols):
        C = self.C
        p = C.psum()
        out = self.ps[p][0:cols, 0:rows]
        ident = self.kc("ident")[0:rows, 0:rows]
        C.P.op("pe", lambda h: h.transpose(out, in_ap, ident), reads=list(in_keys) + [("K",)], writes=[("ps", p)])
        return p

    def conv_silu(self, pre, pk, wcol, bias, out, ok):
        C = self.C
        TB = self.TB
        t, tk = self.scratch()
        C.ts(t, pre[:, 0:TB], wcol(0), None, ALU.mult, None, [pk, ("prm",)], [tk])
        for k in (1, 2, 3):
            C.stt(t, pre[:, k:k + TB], wcol(k), t, ALU.mult, ALU.add, [pk, tk, ("prm",)], [tk])
        if bias is None:
            C.act(out, t, AF.Silu, [tk], [ok])
        else:
            C.act(out, t, AF.Silu, [tk, ("prm",)], [ok], bias=bias)
        C.copy(pre[:, 0:3], pre[:, TB:TB + 3], [pk], [pk])

    def rows_softplus_decay(self, psm, biascol, negA):
        C = self.C
        ps = self.ps
        xs, xk = self.scratch()
        C.ts(xs, ps[psm][:], biascol, None, ALU.add, None, [("ps", psm), ("prm",)], [xk])
        ax, ak = self.scratch()
        C.act(ax, xs, AF.Abs, [xk], [ak])
        C.act(ax, ax, AF.Exp, [ak], [ak], scale=-1.0)
        C.act(ax, ax, AF.Ln, [ak], [ak], bias=1.0)
        sp, sk = self.scratch()
        C.stt(sp, xs, 0.0, ax, ALU.max, ALU.add, [xk, ak], [sk])
        dA, dk = self.scratch()
        C.ts(dA, sp, negA, None, ALU.mult, None, [sk, ("negA",)], [dk])
        acum, ack = self.scratch()
        cm = self.kc("cmask")
        C.P.op("dve", lambda h: h.tensor_tensor_scan(acum, cm, dA, 0.0, ALU.mult, ALU.add), reads=[dk, ("K",)], writes=[ack])
        return (sp, sk), (acum, ack)

    def phase_ssd(self):
        C, ps, TB = self.C, self.ps, self.TB
        P = C.P
        with contextlib.ExitStack() as sa:
            wt = C.sb("wt_ssd", [128, 16, 5 * 128], BF16, sa)
            self.load_w(wt, [10, 11, 12, 13, 14])
            pre = [C.sb("pre%d" % i, [128, 3 + TB], F32, sa) for i in range(3)]
            for i in range(3):
                C.memset(pre[i][:, 0:3], 0.0, [("pre", i)])
            S_T = C.sb("ssdS", [128, 128], F32, sa)
            C.memset(S_T[:], 0.0, [("S", 0), ("S", 1)])
            negA = C.sb("negA", [128, 1], F32, sa)
            C.act(negA[:], self.prm[:, 31:32], AF.Exp, [("prm",)], [("negA",)])
            C.ts(negA[:], negA[:], -1.0, None, ALU.mult, None, [("negA",)], [("negA",)])
            keep = C.sb("ssdkeep", [128, 5, TB], F32, sa)
            nac = C.sb("ssdnac", [128, 2, TB], F32, sa)
            rT = C.sb("ssdrT", [128, 2, 128], F32, sa)
            tmp = C.sb("ssdtmp", [128, 10, 128], F32, sa)
            convw = lambda i: (lambda k: self.prm[:, 12 + i * 4 + k:12 + i * 4 + k + 1])
            for blk in range(self.NBLK):
                t0 = blk * TB
                self.load_x(blk)
                pz = self.proj(wt, 0)
                zs = keep[:, 0, :]
                C.act(zs, ps[pz][:], AF.Silu, [("ps", pz)], [("keep", 0)])
                for i in range(3):
                    pp = self.proj(wt, 1 + i)
                    C.copy(pre[i][:, 3:3 + TB], ps[pp][:], [("ps", pp)], [("pre", i)], eng="act")
                    self.conv_silu(pre[i], ("pre", i), convw(i), self.prm[:, 24 + i:25 + i], keep[:, 1 + i, :], ("keep", 1 + i))
                psm = self.proj(wt, 4)
                (sp, sk), (acum, ack) = self.rows_softplus_decay(psm, self.prm[:, 30:31], negA[:])
                rows = {0: 32, 1: 64}
                for h in range(2):
                    C.ts(nac[:, h, :], acum, self.kc("nsel%d" % rows[h]), None, ALU.mult, None, [ack, ("K",)], [("nac", h)])
                xc, Bc, Cc = keep[:, 1, :], keep[:, 2, :], keep[:, 3, :]
                for ch in range(TB // 128):
                    cs = slice(ch * 128, (ch + 1) * 128)
                    px = self.tr(xc[:, cs], [("keep", 1)], 128, 128)
                    pB = self.tr(Bc[:, cs], [("keep", 2)], 128, 128)
                    pT = self.tr(sp[:, cs], [sk], 128, 128)
                    C.copy(rT[:, 0, :], ps[pT][:, 0:128], [("ps", pT)], [("rT", 0)], eng="act")
                    pT2 = self.tr(acum[:, cs], [ack], 128, 128)
                    C.copy(rT[:, 1, :], ps[pT2][:, 0:128], [("ps", pT2)], [("rT", 1)], eng="act")
                    pbc = C.psum()
                    C.mm(pbc, ps[pbc][:, 0:128], self.kc("El127"), rT[:, 1, :], True, True, [("K",), ("rT", 1)])
                    xdt, Bd0, Bd1 = tmp[:, 0, :], tmp[:, 1, :], tmp[:, 2, :]
                    dl, cd = [], []
                    for h in range(2):
                        r = rows[h]
                        d_, dk_ = self.small()
                        C.tt(d_, ps[pbc][:, r:r + 1], rT[:, 1, r:r + 1], ALU.subtract, [("ps", pbc), ("rT", 1)], [dk_])
                        C.act(d_, d_, AF.Exp, [dk_], [dk_])
                        c_, ck_ = self.small()
                        C.act(c_, ps[pbc][:, r:r + 1], AF.Exp, [("ps", pbc)], [ck_])
                        dl.append((d_, dk_)); cd.append((c_, ck_))
                        C.ts(xdt[:, 64 * h:64 * h + 64], ps[px][:, 64 * h:64 * h + 64], rT[:, 0, r:r + 1], None, ALU.mult, None,
                             [("ps", px), ("rT", 0)], [("tmp", 0, h)])
                        C.ts(tmp[:, 1 + h, :], ps[pB][:, 0:128], d_, None, ALU.mult, None, [("ps", pB), dk_], [("tmp", 1 + h)])
                    pG = C.psum()
                    C.mm(pG, ps[pG][:, 0:128], Bc[:, cs], Cc[:, cs], True, True, [("keep", 2), ("keep", 3)])
                    for h in range(2):
                        r = rows[h]
                        E = self.kc("E%d" % r)
                        pA = C.psum()
                        C.mm(pA, ps[pA][:, 0:128], E, acum[:, cs], True, True, [("K",), ack])
                        pD = C.psum()
                        C.mm(pD, ps[pD][:, 0:128], E, acum[:, cs], True, False, [("K",), ack])
                        C.mm(pD, ps[pD][:, 0:128], nac[:, h, cs], self.kc("ones"), False, True, [("K",), ("nac", h)])
                        Eb, L = tmp[:, 3 + h, :], tmp[:, 5 + h, :]
                        C.act(Eb, ps[pA][:, 0:128], AF.Exp, [("ps", pA)], [("tmp", 3 + h)])
                        C.tt(L, ps[pD][:, 0:128], self.kc("mb_incl"), ALU.add, [("ps", pD), ("K",)], [("tmp", 5 + h)])
                        C.act(L, L, AF.Exp, [("tmp", 5 + h)], [("tmp", 5 + h)])
                        C.tt(L, ps[pG][:, 0:128], L, ALU.mult, [("ps", pG), ("tmp", 5 + h)], [("tmp", 5 + h)])
                        C.tt(Eb, Cc[:, cs], Eb, ALU.mult, [("keep", 3), ("tmp", 3 + h)], [("tmp", 3 + h)])
                    py = C.psum()
                    for h in range(2):
                        hs = slice(64 * h, 64 * h + 64)
                        C.mm(py, ps[py][hs, 0:128], xdt[:, hs], tmp[:, 5 + h, :], True, False, [("tmp", 0, h), ("tmp", 5 + h)])
                        C.mm(py, ps[py][hs, 0:128], S_T[:, hs], tmp[:, 3 + h, :], False, True, [("S", h), ("tmp", 3 + h)])
                    yo = tmp[:, 7 + (ch % 2), :]
                    yk = ("tmp", 7 + (ch % 2))
                    C.stt(yo, xc[:, cs], self.prm[:, 29:30], ps[py][:, 0:128], ALU.mult, ALU.add, [("keep", 1), ("prm",), ("ps", py)], [yk])
                    C.tt(yo, yo, zs[:, cs], ALU.mult, [yk, ("keep", 0)], [yk])
                    C.load(self.out_ap(3, blk, ch * 128, 128), yo, [("dram", self.tag, "ssd", blk, ch)], reads=[yk])
                    pS = C.psum()
                    for h in range(2):
                        hs = slice(64 * h, 64 * h + 64)
                        C.mm(pS, ps[pS][:, hs], tmp[:, 1 + h, :], xdt[:, hs], True, True, [("tmp", 1 + h), ("tmp", 0, h)])
                    for h in range(2):
                        hs = slice(64 * h, 64 * h + 64)
                        C.stt(S_T[:, hs], S_T[:, hs], cd[h][0], ps[pS][:, hs], ALU.mult, ALU.add, [("S", h), cd[h][1], ("ps", pS)], [("S", h)])
            P.barrier()

    def wrap_pi(self, r, rk):
        C = self.C
        m, mk = self.scratch(r.shape[-1])
        C.P.op("dve", lambda h: h.tensor_single_scalar(m, r, float(np.pi), ALU.is_gt), reads=[rk], writes=[mk])
        C.stt(r, m, -TWO_PI, r, ALU.mult, ALU.add, [mk, rk], [rk])
        C.P.op("dve", lambda h: h.tensor_single_scalar(m, r, -float(np.pi), ALU.is_lt), reads=[rk], writes=[mk])
        C.stt(r, m, TWO_PI, r, ALU.mult, ALU.add, [mk, rk], [rk])

    def sincos(self, ang, ak, sin_out, sk, cos_out, ck, itile, ik):
        C = self.C
        w = ang.shape[-1]
        y, yk = self.scratch(w)
        C.ts(y, ang, 1.0 / TWO_PI, None, ALU.mult, None, [ak], [yk])
        C.copy(itile, y, [yk], [ik])
        C.copy(y, itile, [ik], [yk])
        C.stt(ang, y, -CW1, ang, ALU.mult, ALU.add, [yk, ak], [ak])
        C.stt(ang, y, -CW2, ang, ALU.mult, ALU.add, [yk, ak], [ak])
        self.wrap_pi(ang, ak)
        self.wrap_pi(ang, ak)
        C.act(sin_out, ang, AF.Sin, [ak], [sk])
        C.ts(ang, ang, float(np.pi / 2), None, ALU.add, None, [ak], [ak])
        self.wrap_pi(ang, ak)
        C.act(cos_out, ang, AF.Sin, [ak], [ck])

    def phase_ret(self):
        C, ps, TB = self.C, self.ps, self.TB
        P = C.P
        with contextlib.ExitStack() as sa:
            wt = C.sb("wt_ret", [128, 16, 5 * 128], BF16, sa)
            self.load_w(wt, [0, 1, 2, 3, 4])
            R = C.sb("retR", [128, 128], F32, sa)
            C.memset(R[:], 0.0, [("R",)])
            tab = C.sb("rettab", [128, 2, TB], F32, sa)
            qk = C.sb("retqk", [128, 3, TB], F32, sa)
            itile = C.sb("retit", [128, TB], mybir.dt.int32, sa)
            tmp = C.sb("rettmp", [128, 6, 128], F32, sa)
            for blk in range(self.NBLK):
                t0 = blk * TB
                self.load_x(blk)
                ang, ak = self.scratch()
                C.load(ang, self.pos[0:1, t0:t0 + TB].broadcast_to([128, TB]), [ak], eng="pool")
                C.ts(ang, ang, self.kc("rope_inv"), None, ALU.mult, None, [ak, ("K",)], [ak])
                self.sincos(ang, ak, tab[:, 1, :], ("tab", 1), tab[:, 0, :], ("tab", 0), itile[:], ("it",))
                C.ts(tab[:, 1, :], tab[:, 1, :], self.kc("rope_sign"), None, ALU.mult, None, [("tab", 1), ("K",)], [("tab", 1)])
                for i in range(2):
                    pa = self.proj(wt, 2 * i)
                    pb = self.proj(wt, 2 * i + 1)
                    t1, t1k = self.scratch()
                    C.tt(t1, ps[pa][:], tab[:, 0, :], ALU.mult, [("ps", pa), ("tab", 0)], [t1k])
                    C.tt(qk[:, i, :], ps[pb][:], tab[:, 1, :], ALU.mult, [("ps", pb), ("tab", 1)], [("qk", i)])
                    C.tt(qk[:, i, :], qk[:, i, :], t1, ALU.add, [("qk", i), t1k], [("qk", i)])
                pv = self.proj(wt, 4)
                C.copy(qk[:, 2, :], ps[pv][:], [("ps", pv)], [("qk", 2)], eng="act")
                for ch in range(TB // 128):
                    cs = slice(ch * 128, (ch + 1) * 128)
                    pk_ = self.tr(qk[:, 1, cs], [("qk", 1)], 128, 128)
                    kd, vt_, SD, qd = tmp[:, 0, :], tmp[:, 1, :], tmp[:, 2, :], tmp[:, 3, :]
                    C.ts(kd, ps[pk_][:, 0:128], self.kc("ret_kdec"), None, ALU.mult, None, [("ps", pk_), ("K",)], [("tmp", 0)])
                    pv_ = self.tr(qk[:, 2, cs], [("qk", 2)], 128, 128)
                    C.copy(vt_, ps[pv_][:, 0:128], [("ps", pv_)], [("tmp", 1)], eng="act")
                    psc = C.psum()
                    C.mm(psc, ps[psc][:, 0:128], qk[:, 1, cs], qk[:, 0, cs], True, True, [("qk", 1), ("qk", 0)])
                    C.tt(SD, ps[psc][:, 0:128], self.kc("ret_dmat"), ALU.mult, [("ps", psc), ("K",)], [("tmp", 2)])
                    C.tt(qd, qk[:, 0, cs], self.kc("ret_qdec"), ALU.mult, [("qk", 0), ("K",)], [("tmp", 3)])
                    py = C.psum()
                    C.mm(py, ps[py][:, 0:128], vt_, SD, True, False, [("tmp", 1), ("tmp", 2)])
                    C.mm(py, ps[py][:, 0:128], R[:], qd, False, True, [("R",), ("tmp", 3)])
                    yo = tmp[:, 4 + (ch % 2), :]
                    yk = ("tmp", 4 + (ch % 2))
                    C.copy(yo, ps[py][:, 0:128], [("ps", py)], [yk], eng="act")
                    C.load(self.out_ap(0, blk, ch * 128, 128), yo, [("dram", self.tag, "ret", blk, ch)], reads=[yk])
                    pR = C.psum()
                    C.mm(pR, ps[pR][:, 0:128], kd, vt_, True, True, [("tmp", 0), ("tmp", 1)])
                    C.stt(R[:], R[:], self.kc("ret_gc"), ps[pR][:, 0:128], ALU.mult, ALU.add, [("R",), ("K",), ("ps", pR)], [("R",)])
            P.barrier()

    def phase_gdn(self):
        C, ps, TB = self.C, self.ps, self.TB
        P = C.P
        CH = 64
        with contextlib.ExitStack() as sa:
            wt = C.sb("wt_gdn", [128, 16, 5 * 128], BF16, sa)
            self.load_w(wt, [5, 6, 7, 8, 14])
            nw = C.sb("gdn_nw", [128, 128], F32, sa)
            C.load(nw[:], self.nw_d, [("nw",)])
            pre = [C.sb("gpre%d" % i, [128, 3 + TB], F32, sa) for i in range(3)]
            for i in range(3):
                C.memset(pre[i][:, 0:3], 0.0, [("pre", i)])
            S = C.sb("gdnS", [128, 128], F32, sa)
            C.memset(S[:], 0.0, [("S",)])
            negA = C.sb("gnegA", [128, 1], F32, sa)
            C.act(negA[:], self.prm[:, 31:32], AF.Exp, [("prm",)], [("negA",)])
            C.ts(negA[:], negA[:], -1.0, None, ALU.mult, None, [("negA",)], [("negA",)])
            keep = C.sb("gkeep", [128, 6, TB], F32, sa)
            G_ = C.sb("gtmp", [128, 28, 128], F32, sa)
            junk = C.sb("gjunk", [128, 128], F32, sa)
            convw = lambda i: (lambda k: self.prm[:, i * 4 + k:i * 4 + k + 1])

            def g(i, rows=128, cols=128):
                return G_[0:rows, i, 0:cols], ("g", i)

            for blk in range(self.NBLK):
                t0 = blk * TB
                self.load_x(blk)
                for i in range(3):
                    pp = self.proj(wt, i)
                    C.copy(pre[i][:, 3:3 + TB], ps[pp][:], [("ps", pp)], [("pre", i)], eng="act")
                    self.conv_silu(pre[i], ("pre", i), convw(i), None, keep[:, i, :], ("keep", i))
                pz = self.proj(wt, 3)
                C.act(keep[:, 3, :], ps[pz][:], AF.Silu, [("ps", pz)], [("keep", 3)])
                psm = self.proj(wt, 4)
                C.act(keep[:, 4, :], ps[psm][:], AF.Sigmoid, [("ps", psm)], [("keep", 4)])
                (sp, sk), (acum, ack) = self.rows_softplus_decay(psm, self.prm[:, 30:31], negA[:])
                C.ts(keep[:, 5, :], acum, self.kc("nsel0"), None, ALU.mult, None, [ack, ("K",)], [("keep", 5)])
                qc, kc_, vc, zs, sig, nac0 = (keep[:, i, :] for i in range(6))
                for ch in range(TB // CH):
                    cs = slice(ch * CH, (ch + 1) * CH)
                    pT = self.tr(acum[:, cs], [ack], 128, CH)
                    acT, acTk = g(0, CH)
                    C.copy(acT, ps[pT][0:CH, 0:128], [("ps", pT)], [acTk], eng="act")
                    pT = self.tr(sig[:, cs], [("keep", 4)], 128, CH)
                    sgT, sgTk = g(1, CH)
                    C.copy(sgT, ps[pT][0:CH, 0:128], [("ps", pT)], [sgTk], eng="act")
                    pbc = C.psum()
                    C.mm(pbc, ps[pbc][:, 0:128], self.kc("El63", CH), acT, True, True, [("K",), acTk])
                    sm, smk = g(2)
                    C.act(sm[0:CH, 0:1], acT[:, 0:1], AF.Exp, [acTk], [smk])
                    C.tt(sm[0:CH, 1:2], sm[0:CH, 0:1], sgT[:, 1:2], ALU.mult, [smk, sgTk], [smk])
                    C.tt(sm[0:CH, 2:3], ps[pbc][0:CH, 0:1], acT[:, 0:1], ALU.subtract, [("ps", pbc), acTk], [smk])
                    C.act(sm[0:CH, 2:3], sm[0:CH, 2:3], AF.Exp, [smk], [smk])
                    C.act(sm[:, 3:4], ps[pbc][:, 0:1], AF.Exp, [("ps", pbc)], [smk])
                    pT = self.tr(zs[:, cs], [("keep", 3)], 128, CH)
                    zt, ztk = g(3, CH)
                    C.copy(zt, ps[pT][0:CH, 0:128], [("ps", pT)], [ztk], eng="act")
                    pT = self.tr(vc[:, cs], [("keep", 2)], 128, CH)
                    vb, vbk = g(4, CH)
                    C.ts(vb, ps[pT][0:CH, 0:128], sgT[:, 1:2], None, ALU.mult, None, [("ps", pT), sgTk], [vbk])
                    qn, qnk = g(5, CH)
                    kn, knk = g(6, CH)
                    for (src_i, dst, dk_, col, scl) in ((0, qn, qnk, 4, 128.0 ** -0.5), (1, kn, knk, 5, 1.0)):
                        pT = self.tr(keep[:, src_i, cs], [("keep", src_i)], 128, CH)
                        C.P.op("act", lambda h, pT=pT, col=col: h.activation(junk[0:CH, :], ps[pT][0:CH, 0:128], AF.Square,
                                                                             accum_out=sm[0:CH, col:col + 1]),
                               reads=[("ps", pT)], writes=[smk, ("junk",)])
                        C.ts(sm[0:CH, col:col + 1], sm[0:CH, col:col + 1], RMS_EPS, None, ALU.add, None, [smk], [smk])
                        C.act(sm[0:CH, col:col + 1], sm[0:CH, col:col + 1], AF.Sqrt, [smk], [smk])
                        C.recip(sm[0:CH, col:col + 1], sm[0:CH, col:col + 1], [smk], [smk])
                        C.ts(dst, ps[pT][0:CH, 0:128], sm[0:CH, col:col + 1], scl, ALU.mult, ALU.mult, [("ps", pT), smk], [dk_])
                    pT = self.tr(qn, [qnk], CH, 128)
                    qnT, qnTk = g(7, 128, CH)
                    C.copy(qnT, ps[pT][:, 0:CH], [("ps", pT)], [qnTk], eng="act")
                    pT = self.tr(kn, [knk], CH, 128)
                    knT, knTk = g(8, 128, CH)
                    C.copy(knT, ps[pT][:, 0:CH], [("ps", pT)], [knTk], eng="act")
                    kb, kbk = g(9, CH)
                    C.ts(kb, kn, sm[0:CH, 1:2], None, ALU.mult, None, [knk, smk], [kbk])
                    kt, ktk = g(10, CH)
                    C.ts(kt, kn, sm[0:CH, 2:3], None, ALU.mult, None, [knk, smk], [ktk])
                    pD = C.psum()
                    C.mm(pD, ps[pD][0:CH, 0:CH], self.kc("E0")[:, 0:CH], acum[:, cs], True, False, [("K",), ack])
                    C.mm(pD, ps[pD][0:CH, 0:CH], nac0[:, cs], self.kc("ones")[:, 0:CH], False, True, [("K",), ("keep", 5)])
                    Gm, Gk = g(11, CH, CH)
                    C.tt(Gm, ps[pD][0:CH, 0:CH], self.kc("mb_incl", CH)[:, 0:CH], ALU.add, [("ps", pD), ("K",)], [Gk])
                    C.act(Gm, Gm, AF.Exp, [Gk], [Gk])
                    pE = C.psum()
                    C.mm(pE, ps[pE][:, 0:CH], self.kc("E0"), acum[:, cs], True, True, [("K",), ack])
                    qd, qdk = g(12, 128, CH)
                    C.act(qd, ps[pE][:, 0:CH], AF.Exp, [("ps", pE)], [qdk])
                    C.tt(qd, qd, qnT, ALU.mult, [qdk, qnTk], [qdk])
                    pB = C.psum()
                    C.mm(pB, ps[pB][0:CH, 0:CH], self.kc("E1")[:, 0:CH], sig[:, cs], True, True, [("K",), ("keep", 4)])
                    pK = C.psum()
                    C.mm(pK, ps[pK][0:CH, 0:CH], knT, knT, True, True, [knTk])
                    X, Xk = g(13, CH, CH)
                    C.tt(X, ps[pK][0:CH, 0:CH], Gm, ALU.mult, [("ps", pK), Gk], [Xk])
                    C.tt(X, X, ps[pB][0:CH, 0:CH], ALU.mult, [Xk, ("ps", pB)], [Xk])
                    C.stt(X, X, -1.0, self.kc("strict", CH)[:, 0:CH], ALU.mult, ALU.mult, [Xk, ("K",)], [Xk])
                    pQ = C.psum()
                    C.mm(pQ, ps[pQ][0:CH, 0:CH], knT, qnT, True, True, [knTk, qnTk])
                    qkT, qkTk = g(14, CH, CH)
                    C.tt(qkT, ps[pQ][0:CH, 0:CH], Gm, ALU.mult, [("ps", pQ), Gk], [qkTk])
                    pT = self.tr(X, [Xk], CH, CH)
                    XT, XTk = g(15, CH, CH)
                    C.copy(XT, ps[pT][0:CH, 0:CH], [("ps", pT)], [XTk], eng="act")
                    Pm, Pk = g(16, CH, CH)
                    C.tt(Pm, X, self.kc("ident", CH)[:, 0:CH], ALU.add, [Xk, ("K",)], [Pk])
                    cur = (X, Xk, XT, XTk)
                    for lev in range(1, 6):
                        Xp, Xpk, XTp, XTpk = cur
                        nXT, nXTk = g(17 + 2 * (lev % 2), CH, CH)
                        nX, nXk = g(18 + 2 * (lev % 2), CH, CH)
                        p1 = C.psum()
                        C.mm(p1, ps[p1][0:CH, 0:CH], Xp, XTp, True, True, [Xpk, XTpk])
                        C.copy(nXT, ps[p1][0:CH, 0:CH], [("ps", p1)], [nXTk], eng="act")
                        if lev < 5:
                            p2 = C.psum()
                            C.mm(p2, ps[p2][0:CH, 0:CH], XTp, Xp, True, True, [Xpk, XTpk])
                            C.copy(nX, ps[p2][0:CH, 0:CH], [("ps", p2)], [nXk])
                        p3 = C.psum()
                        C.mm(p3, ps[p3][0:CH, 0:CH], nXT, Pm, True, True, [nXTk, Pk])
                        C.tt(Pm, Pm, ps[p3][0:CH, 0:CH], ALU.add, [Pk, ("ps", p3)], [Pk])
                        cur = (nX, nXk, nXT, nXTk)
                    pW = C.psum()
                    C.mm(pW, ps[pW][:, 0:CH], kb, Pm, True, True, [kbk, Pk])
                    wT, wTk = g(21, 128, CH)
                    C.copy(wT, ps[pW][:, 0:CH], [("ps", pW)], [wTk], eng="act")
                    pU = C.psum()
                    C.mm(pU, ps[pU][0:CH, 0:128], Pm, vb, True, True, [Pk, vbk])
                    us, usk = g(22, CH)
                    C.copy(us, ps[pU][0:CH, 0:128], [("ps", pU)], [usk])
                    pw = C.psum()
                    C.mm(pw, ps[pw][0:CH, 0:128], wT, S[:], True, True, [wTk, ("S",)])
                    vn, vnk = g(23, CH)
                    C.tt(vn, us, ps[pw][0:CH, 0:128], ALU.subtract, [usk, ("ps", pw)], [vnk])
                    po = C.psum()
                    C.mm(po, ps[po][0:CH, 0:128], qd, S[:], True, False, [qdk, ("S",)])
                    C.mm(po, ps[po][0:CH, 0:128], qkT, vn, False, True, [qkTk, vnk])
                    pS = C.psum()
                    C.mm(pS, ps[pS][:, 0:128], kt, vn, True, True, [ktk, vnk])
                    C.stt(S[:], S[:], sm[:, 3:4], ps[pS][:, 0:128], ALU.mult, ALU.add, [("S",), smk, ("ps", pS)], [("S",)])
                    C.P.op("act", lambda h, po=po: h.activation(junk[0:CH, :], ps[po][0:CH, 0:128], AF.Square, accum_out=sm[0:CH, 6:7]),
                           reads=[("ps", po)], writes=[smk, ("junk",)])
                    C.ts(sm[0:CH, 6:7], sm[0:CH, 6:7], 1.0 / 128, RMS_EPS, ALU.mult, ALU.add, [smk], [smk])
                    C.act(sm[0:CH, 6:7], sm[0:CH, 6:7], AF.Sqrt, [smk], [smk])
                    C.recip(sm[0:CH, 6:7], sm[0:CH, 6:7], [smk], [smk])
                    on, onk = g(24 + (ch % 2), CH)
                    C.stt(on, ps[po][0:CH, 0:128], sm[0:CH, 6:7], nw[0:CH, :], ALU.mult, ALU.mult, [("ps", po), smk, ("nw",)], [onk])
                    C.tt(on, on, zt, ALU.mult, [onk, ztk], [onk])
                    pT = self.tr(on, [onk], CH, 128)
                    onT, onTk = g(26 + (ch % 2), 128, CH)
                    C.copy(onT, ps[pT][:, 0:CH], [("ps", pT)], [onTk], eng="act")
                    C.load(self.out_ap(1, blk, ch * CH, CH), onT, [("dram", self.tag, "gdn", blk, ch)], reads=[onTk])
            P.barrier()

    def phase_s5(self):
        C, ps, TB = self.C, self.ps, self.TB
        P = C.P
        NG = 8
        with contextlib.ExitStack() as sa:
            wt = C.sb("wt_s5", [128, 16, 128], BF16, sa)
            self.load_w(wt, [9])
            sm_ = C.sb("s5m", [128, 4, NG, 128], F32, sa)
            C.load(sm_[:], self.s5m_d, [("s5m",)])
            q = C.sb("s5q", [128, 16, NG], F32, sa)
            qi = C.sb("s5qi", [128, NG], mybir.dt.int32, sa)
            tabs = C.sb("s5tab", [128, NG, 4, TB], F32, sa)
            Rg = C.sb("s5R", [128, NG, 128], F32, sa)
            carry = C.sb("s5carry", [128, NG], F32, sa)
            uT = C.sb("s5u", [128, TB], F32, sa)
            itile = C.sb("s5it", [128, TB], mybir.dt.int32, sa)
            C.memset(carry[:], 0.0, [("carry", g) for g in range(NG)])
            lre, lim, lst = self.prm[:, 33:41], self.prm[:, 41:49], self.prm[:, 49:57]
            QK = ("q",)
            STEP, MAG, TH, COS, SIN, ABR, ABI, DEN, FRE, FIM, T0, T1, F2, G2, SP = range(15)
            qq = lambda i: q[:, i, :]
            C.act(qq(STEP), lst, AF.Exp, [("prm",)], [QK])
            C.tt(qq(MAG), lre, qq(STEP), ALU.mult, [("prm",), QK], [QK])
            C.act(qq(MAG), qq(MAG), AF.Exp, [QK], [QK])
            C.tt(qq(TH), lim, qq(STEP), ALU.mult, [("prm",), QK], [QK])
            C.copy(qq(T0), qq(TH), [QK], [QK])
            self.sincos(qq(T0), QK, qq(SIN), QK, qq(COS), QK, qi[:], ("qi",))
            C.tt(qq(ABR), qq(MAG), qq(COS), ALU.mult, [QK], [QK])
            C.tt(qq(ABI), qq(MAG), qq(SIN), ALU.mult, [QK], [QK])
            C.tt(qq(DEN), lre, lre, ALU.mult, [("prm",)], [QK])
            C.tt(qq(T0), lim, lim, ALU.mult, [("prm",)], [QK])
            C.tt(qq(DEN), qq(DEN), qq(T0), ALU.add, [QK], [QK])
            C.recip(qq(DEN), qq(DEN), [QK], [QK])
            C.ts(qq(T0), qq(ABR), -1.0, None, ALU.add, None, [QK], [QK])
            C.tt(qq(FRE), qq(T0), lre, ALU.mult, [QK, ("prm",)], [QK])
            C.tt(qq(T1), qq(ABI), lim, ALU.mult, [QK, ("prm",)], [QK])
            C.tt(qq(FRE), qq(FRE), qq(T1), ALU.add, [QK], [QK])
            C.tt(qq(FRE), qq(FRE), qq(DEN), ALU.mult, [QK], [QK])
            C.tt(qq(FIM), qq(ABI), lre, ALU.mult, [QK, ("prm",)], [QK])
            C.tt(qq(T1), qq(T0), lim, ALU.mult, [QK, ("prm",)], [QK])
            C.tt(qq(FIM), qq(FIM), qq(T1), ALU.subtract, [QK], [QK])
            C.tt(qq(FIM), qq(FIM), qq(DEN), ALU.mult, [QK], [QK])
            C.ts(qq(F2), qq(FIM), self.kc("rope_sign"), None, ALU.mult, None, [QK, ("K",)], [QK])
            C.ts(qq(G2), qq(FRE), self.kc("sgn_pm"), None, ALU.mult, None, [QK, ("K",)], [QK])
            for g in range(NG):
                cosT, sinT, TA, TBt = (tabs[:, g, i, :] for i in range(4))
                tk = lambda i: ("tab", g, i)
                ang, ak = self.scratch()
                C.ts(ang, self.kc("tau1"), q[:, TH, g:g + 1], None, ALU.mult, None, [("K",), QK], [ak])
                self.sincos(ang, ak, sinT, tk(1), cosT, tk(0), itile[:], ("it",))
                C.ts(TA, cosT, q[:, FRE, g:g + 1], None, ALU.mult, None, [tk(0), QK], [tk(2)])
                C.stt(TA, sinT, q[:, FIM, g:g + 1], TA, ALU.mult, ALU.add, [tk(1), tk(2), QK], [tk(2)])
                C.ts(TBt, cosT, q[:, F2, g:g + 1], None, ALU.mult, None, [tk(0), QK], [tk(3)])
                C.stt(TBt, sinT, q[:, G2, g:g + 1], TBt, ALU.mult, ALU.add, [tk(1), tk(3), QK], [tk(3)])
                C.ts(q[:, SP, g:g + 1], sinT[:, TB - 1:TB], self.kc("sgn_pm"), None, ALU.mult, None, [tk(1), ("K",)], [QK])
                C.ts(Rg[:, g, :], self.kc("ident"), cosT[:, TB - 1:TB], None, ALU.mult, None, [tk(0), ("K",)], [("Rg", g)])
                C.stt(Rg[:, g, :], self.kc("shift64"), q[:, SP, g:g + 1], Rg[:, g, :], ALU.mult, ALU.add, [("K",), QK, ("Rg", g)], [("Rg", g)])
                C.ts(sm_[:, 2, g, :], sm_[:, 2, g, :], self.kc("sgn_pm"), None, ALU.mult, None, [("s5m",), ("K",)], [("s5m",)])
                C.ts(sm_[:, 3, g, :], sm_[:, 3, g, :], -1.0, None, ALU.mult, None, [("s5m",)], [("s5m",)])
            for blk in range(self.NBLK):
                t0 = blk * TB
                self.load_x(blk)
                pu = self.proj(wt, 0)
                C.copy(uT[:], ps[pu][:], [("ps", pu)], [("uT",)], eng="act")
                py = C.psum_reserve()
                for g in range(NG):
                    cosT, sinT, TA, TBt = (tabs[:, g, i, :] for i in range(4))
                    tk = lambda i: ("tab", g, i)
                    p1 = C.psum()
                    C.mm(p1, ps[p1][:], sm_[:, 0, g, :], uT[:], True, True, [("s5m",), ("uT",)])
                    p2 = C.psum()
                    C.mm(p2, ps[p2][:], sm_[:, 1, g, :], uT[:], True, True, [("s5m",), ("uT",)])
                    v, vk = self.scratch()
                    v2, v2k = self.scratch()
                    C.tt(v, ps[p1][:], TA, ALU.mult, [("ps", p1), tk(2)], [vk])
                    C.tt(v2, ps[p2][:], TBt, ALU.mult, [("ps", p2), tk(3)], [v2k])
                    C.tt(v, v, v2, ALU.add, [vk, v2k], [vk], eng="pool")
                    w, wk = self.scratch()
                    mb = q[:, MAG, g:g + 1].broadcast_to([128, TB])
                    C.P.op("dve", lambda h, w=w, mb=mb, v=v, g=g: h.tensor_tensor_scan(w, mb, v, carry[:, g:g + 1], ALU.mult, ALU.add),
                           reads=[vk, QK, ("carry", g)], writes=[wk])
                    z1, z1k = self.scratch()
                    C.tt(z1, w, cosT, ALU.mult, [wk, tk(0)], [z1k])
                    C.tt(v2, w, sinT, ALU.mult, [wk, tk(1)], [v2k], eng="pool")
                    C.mm(py, ps[py][:], sm_[:, 2, g, :], z1, g == 0, False, [("s5m",), z1k])
                    C.mm(py, ps[py][:], sm_[:, 3, g, :], v2, False, g == NG - 1, [("s5m",), v2k])
                    pc = C.psum()
                    C.mm(pc, ps[pc][:, 0:1], Rg[:, g, :], w[:, TB - 1:TB], True, True, [("Rg", g), wk])
                    C.copy(carry[:, g:g + 1], ps[pc][:, 0:1], [("ps", pc)], [("carry", g)], eng="act")
                yo, yk = self.scratch()
                C.stt(yo, uT[:], self.prm[:, 32:33], ps[py][:], ALU.mult, ALU.add, [("uT",), ("prm",), ("ps", py)], [yk])
                C.psum_release(py)
                C.load(self.out_ap(2, blk, 0, TB), yo, [("dram", self.tag, "s5", blk)], reads=[yk])
            P.barrier()


def mix_inputs(l, c, inp, xT_full, nt):
    w = inp["w_in"][l]
    h, e = c // 2, c % 2
    Z = np.zeros((D, 128), np.float32)
    q = w[:, h * 128:(h + 1) * 128]
    k = w[:, 512 + h * 128:512 + (h + 1) * 128]
    perm = lambda a: np.concatenate([a[:, 64:], a[:, :64]], axis=1)
    v = w[:, 1024 + h * 256 + e * 128:1024 + h * 256 + (e + 1) * 128]
    g0 = 3072
    gq = w[:, g0 + c * 128:g0 + (c + 1) * 128]
    gk = w[:, g0 + 1024 + c * 128:g0 + 1024 + (c + 1) * 128]
    gv = w[:, g0 + 2048 + c * 128:g0 + 2048 + (c + 1) * 128]
    gz = w[:, g0 + 3072 + c * 128:g0 + 3072 + (c + 1) * 128]
    ga = w[:, g0 + 4096 + c]
    gb = w[:, g0 + 4104 + c]
    s0 = g0 + 4112
    su = w[:, s0 + c * 128:s0 + (c + 1) * 128]
    m0 = s0 + 1024
    mz = w[:, m0 + c * 128:m0 + (c + 1) * 128]
    mx = w[:, m0 + 1024 + c * 128:m0 + 1024 + (c + 1) * 128]
    grp = c // 4
    mB = w[:, m0 + 2048 + grp * 128:m0 + 2048 + (grp + 1) * 128]
    mC = w[:, m0 + 2304 + grp * 128:m0 + 2304 + (grp + 1) * 128]
    mdt = w[:, m0 + 2560 + 2 * c:m0 + 2560 + 2 * c + 2]
    small = Z.copy()
    small[:, 0] = ga; small[:, 1] = gb; small[:, 32] = mdt[:, 0]; small[:, 64] = mdt[:, 1]
    wmix = np.concatenate([q, perm(q), k, perm(k), v, gq, gk, gv, gz, su, mz, mx, mB, mC, small], axis=1)
    prm = np.zeros((128, 64), np.float32)
    gcw = inp["gdn_conv_w"][l]
    for i in range(3):
        prm[:, i * 4:(i + 1) * 4] = gcw[i * 1024 + c * 128:i * 1024 + (c + 1) * 128]
    scw = inp["ssd_conv_w"][l]
    scb = inp["ssd_conv_b"][l]
    rowsl = [slice(c * 128, (c + 1) * 128), slice(1024 + grp * 128, 1024 + (grp + 1) * 128),
             slice(1280 + grp * 128, 1280 + (grp + 1) * 128)]
    for i in range(3):
        prm[:, 12 + i * 4:12 + (i + 1) * 4] = scw[rowsl[i]]
        prm[:, 24 + i] = scb[rowsl[i]]
    prm[:64, 29] = inp["ssd_d"][l][2 * c]; prm[64:, 29] = inp["ssd_d"][l][2 * c + 1]
    prm[0, 30] = inp["gdn_dt_bias"][l][c]; prm[32, 30] = inp["ssd_dt_bias"][l][2 * c]; prm[64, 30] = inp["ssd_dt_bias"][l][2 * c + 1]
    prm[0, 31] = inp["gdn_a_log"][l][c]; prm[32, 31] = inp["ssd_a_log"][l][2 * c]; prm[64, 31] = inp["ssd_a_log"][l][2 * c + 1]
    prm[:, 32] = inp["s5_d"][l][c * 128:(c + 1) * 128]
    gs = slice(8 * c, 8 * c + 8)
    dup = lambda a: np.concatenate([a, a], axis=0)
    prm[:, 33:41] = dup(inp["s5_lam_re"][l][gs].T)
    prm[:, 41:49] = dup(inp["s5_lam_im"][l][gs].T)
    prm[:, 49:57] = np.broadcast_to(inp["s5_log_step"][l][gs][None, :], (128, 8))
    nw = np.ascontiguousarray(np.broadcast_to(inp["gdn_norm_w"][l][None, :], (128, 128))).astype(np.float32)
    s5m = np.zeros((128, 4, 8, 128), np.float32)
    bre, bim = inp["s5_b_re"][l][gs], inp["s5_b_im"][l][gs]
    cre, cim = inp["s5_c_re"][l][gs], inp["s5_c_im"][l][gs]
    for g in range(8):
        r = slice(16 * g, 16 * g + 16)
        s5m[r, 0, g, 0:64] = bre[g].T; s5m[r, 0, g, 64:128] = bim[g].T
        s5m[r, 1, g, 0:64] = bim[g].T; s5m[r, 1, g, 64:128] = bre[g].T
        s5m[0:64, 2, g, r] = cre[g].T; s5m[64:128, 2, g, r] = cim[g].T
        s5m[0:64, 3, g, r] = cim[g].T; s5m[64:128, 3, g, r] = cre[g].T
    if xT_full is None:
        return {"wmix": np.ascontiguousarray(wmix), "prm": prm, "gdn_nw": nw, "s5m": s5m}
    return {"xT": np.ascontiguousarray(xT_full[:, :nt]), "pos": np.ascontiguousarray(inp["positions"][:, :nt]).astype(np.int32),
            "wmix": np.ascontiguousarray(wmix), "consts": make_consts(c), "prm": prm, "gdn_nw": nw, "s5m": s5m}


YROWS = 4 * 128 * 16


def build_fused(debug=None):
    nc = bass.Bass("TRN2", target_bir_lowering=False)
    I32 = mybir.dt.int32
    din = lambda name, shape, dt=F32: nc.dram_tensor(name, shape, dt, kind="ExternalInput").ap()
    xT_in = din("xT", [D, NT])
    pos = din("pos", [1, T], I32)
    consts_d = din("consts", [128, NCONST])
    gidx_d = din("gidx", [128, 64], I32)
    L = []
    for l in range(DEPTH):
        s = str(l)
        if debug == "mix":
            if l == 0:
                L.append(dict(wmix=din("wmix" + s, [D, 15 * 128]), prm=din("prm" + s, [128, 64]), nw=din("gdn_nw" + s, [128, 128]),
                              s5m=din("s5m" + s, [128, 4, 8, 128])))
            continue
        L.append(dict(
            wmix=din("wmix" + s, [D, 15 * 128]), prm=din("prm" + s, [128, 64]), nw=din("gdn_nw" + s, [128, 128]),
            s5m=din("s5m" + s, [128, 4, 8, 128]), w_gate=din("w_gate" + s, [D, 4 * D]), w_rg=din("w_rg" + s, [D, 1024]),
            wb=[din("wb%d_%s" % (b, s), [1024, D]) for b in range(4)], w_out=din("w_out" + s, [D, D]),
            w_up=din("w_up" + s, [D, DFF]), w_down=din("w_down" + s, [DFF, D]), glu_w=din("glu_w" + s, [1024, 1024]),
            vecs=din("vecs" + s, [128, 80])))
    outT = nc.dram_tensor("outT", [D, NT], F32, kind="ExternalOutput").ap()
    dbgT = nc.dram_tensor("dbgT", [4, 1024, NT], F32, kind="ExternalOutput").ap() if debug == "mix" else None
    xb_loc = [nc.dram_tensor("xb_loc%d" % l, [D, NT], BF16) for l in range(DEPTH)]
    xg = [nc.dram_tensor("xg%d" % l, [NCORES * D, NT], BF16) for l in range(DEPTH)]
    ybuf = [nc.dram_tensor("ybuf%d" % l, [YROWS, 512], F32) for l in range(DEPTH)]
    yg = [nc.dram_tensor("yg%d" % l, [NCORES * YROWS, 512], F32) for l in range(DEPTH)]
    xres = nc.dram_tensor("xres", [D, NT], F32).ap()
    x1s = nc.dram_tensor("x1s", [D, NT], F32).ap()

    C = Ctx(nc)
    P = C.P
    gidx = C.sb("gidx", [128, 64], I32)
    C.load(gidx[:], gidx_d, [("gidx",)])
    with contextlib.ExitStack() as s0:
        t_ = C.sb("x0b", [128, 16, 512], BF16, s0)
        for tb in range(NT // 512):
            C.load(t_[:], xT_in[:, tb * 512:(tb + 1) * 512].rearrange("(k p) t -> p k t", p=128), [("x0b",)], eng="pool")
            C.load(xb_loc[0].ap()[:, tb * 512:(tb + 1) * 512].rearrange("(k p) t -> p k t", p=128), t_[:],
                   [("dram", "xb0", tb)], reads=[("x0b",)])
        P.barrier()
    for l in range(DEPTH):
        tag = "L%d" % l
        P.cc("AllGather", [xb_loc[l].ap().opt()], [xg[l].ap().opt()], writes=[("xg", tag)])
        with contextlib.ExitStack() as stg:
            mx = Mix(T, (), standalone=False)
            mx.xg = xg[l].ap()
            mx.ybuf = ybuf[l].ap().rearrange("(m p b) t -> m p b t", m=4, p=128)
            mx.pos = pos
            mx.wmix, mx.consts_d, mx.prm_d, mx.nw_d, mx.s5m_d = L[l]["wmix"], consts_d, L[l]["prm"], L[l]["nw"], L[l]["s5m"]
            mx.run(C, stg, ("ssd", "ret", "gdn", "s5"), tag)
            P.barrier()
        P.cc("AllGather", [ybuf[l].ap().opt()], [yg[l].ap().opt()], writes=[("yg", tag)])
        if debug == "mix":
            with contextlib.ExitStack() as stg:
                dt_ = C.sb("dbgt", [128, 4, 512], F32, stg)
                i_ = 0
                for m in range(4):
                    for ct in range(8):
                        for tb in range(2):
                            col = (ct * 4 + m) * 2 + tb
                            P.gather(dt_[:, i_ % 4, :], yg[l].ap()[:, :], gidx[:, col:col + 1], NCORES * YROWS,
                                     reads=[("yg", tag), ("gidx",)], writes=[("dbgt", i_ % 4)])
                            C.load(dbgT[m, ct * 128:(ct + 1) * 128, tb * 512:(tb + 1) * 512], dt_[:, i_ % 4, :],
                                   [("dram", "dbg", m, ct, tb)], reads=[("dbgt", i_ % 4)])
                            i_ += 1
                P.barrier()
            break
        with contextlib.ExitStack() as stg:
            E = dict(L[l])
            E["xT"] = xT_in if l == 0 else xres
            E["xb"] = xb_loc[l].ap()
            E["x1s"] = x1s
            last = l == DEPTH - 1
            E["out"] = outT if last else xres
            E["out_bf"] = None if last else xb_loc[l + 1].ap()
            yg2 = yg[l].ap()

            def load_y(m, ct, tb, dst, key, yg2=yg2, tag=tag):
                col = (ct * 4 + m) * 2 + tb
                P.gather(dst, yg2[:, :], gidx[:, col:col + 1], NCORES * YROWS, reads=[("yg", tag), ("gidx",)], writes=[key])

            E["load_y"] = load_y
            dense_layer(C, stg, E, NT, tag)
            P.barrier()
    C.finish([k for k in list(P.last_w.keys()) if k[0] == "dram"])
    return nc


def fused_inputs(c, inp):
    xT = np.ascontiguousarray(inp["x"][0].T[:, c * NT:(c + 1) * NT])
    m = {"xT": xT, "pos": np.ascontiguousarray(inp["positions"]).astype(np.int32), "consts": make_consts(c)}
    p = np.arange(128)
    gidx = np.zeros((128, 64), np.int32)
    for r in range(NCORES):
        for mm_ in range(4):
            for tb in range(2):
                gidx[:, (r * 4 + mm_) * 2 + tb] = r * YROWS + (mm_ * 128 + p) * 16 + (c * 2 + tb)
    m["gidx"] = gidx
    return m


def fused_layer_inputs(l, c, inp, shared):
    s = str(l)
    mi = mix_inputs(l, c, inp, None, T)
    m = {"wmix" + s: mi["wmix"], "prm" + s: mi["prm"], "gdn_nw" + s: mi["gdn_nw"], "s5m" + s: mi["s5m"]}
    if l not in shared:
        d = dense_inputs(l, 0, inp, None, None, None, None, None)
        shared[l] = {"w_gate" + s: d["w_gate"], "w_rg" + s: d["w_rg"], "w_out" + s: d["w_out"], "w_up" + s: d["w_up"],
                     "w_down" + s: d["w_down"], "glu_w" + s: d["glu_w"], "vecs" + s: d["vecs"],
                     **{"wb%d_%s" % (b, s): d["wb%d" % b] for b in range(4)}}
    m.update(shared[l])
    return m


_PROGS = {}


def _get_progs():
    if "mix" not in _PROGS:
        _PROGS["mix"] = Mix(T, ("ssd", "ret", "gdn", "s5")).nc
        _PROGS["dense"] = build_dense(NT)
    return _PROGS["mix"], _PROGS["dense"]


def kernel_unfused(**inputs):
    inp = {k: np.asarray(v) for k, v in inputs.items()}
    ncm, ncd = _get_progs()
    cores = list(range(NCORES))
    xT = np.ascontiguousarray(inp["x"][0].T)
    for l in range(DEPTH):
        maps = [mix_inputs(l, c, inp, xT, T) for c in cores]
        res = run_bass_kernel_spmd(ncm, maps, core_ids=cores).results
        retT = np.concatenate([res[c]["o_ret"] for c in cores], axis=0)
        gdnT = np.concatenate([res[c]["o_gdn"] for c in cores], axis=0)
        s5T = np.concatenate([res[c]["o_s5"] for c in cores], axis=0)
        ssdT = np.concatenate([res[c]["o_ssd"] for c in cores], axis=0)
        shared = dense_inputs(l, 0, inp, xT, retT, gdnT, s5T, ssdT)
        maps = []
        for c in cores:
            m = dict(shared)
            sl = slice(c * NT, (c + 1) * NT)
            m["xT"] = np.ascontiguousarray(xT[:, sl])
            m["retT"] = np.ascontiguousarray(retT[:, sl]); m["gdnT"] = np.ascontiguousarray(gdnT[:, sl])
            m["s5T"] = np.ascontiguousarray(s5T[:, sl]); m["ssdT"] = np.ascontiguousarray(ssdT[:, sl])
            maps.append(m)
        res = run_bass_kernel_spmd(ncd, maps, core_ids=cores).results
        xT = np.concatenate([res[c]["outT"] for c in cores], axis=1)
    return np.ascontiguousarray(xT.T)[None].astype(np.float32)


def kernel(**inputs):
    inp = {k: np.asarray(v) for k, v in inputs.items()}
    if "fused" not in _PROGS:
        _PROGS["fused"] = build_fused()
    nc = _PROGS["fused"]
    cores = list(range(NCORES))
    shared = {}
    maps = []
    for c in cores:
        m = fused_inputs(c, inp)
        for l in range(DEPTH):
            m.update(fused_layer_inputs(l, c, inp, shared))
        maps.append(m)
    res = run_bass_kernel_spmd(nc, maps, core_ids=cores).results
    xT = np.concatenate([res[c]["outT"] for c in cores], axis=1)
    return np.ascontiguousarray(xT.T)[None].astype(np.float32)
```

```python
import contextlib
import numpy as np
import concourse.bass as bass
import concourse.mybir as mybir
from concourse.bass_utils import run_bass_kernel_spmd

F32 = mybir.dt.float32
BF16 = mybir.dt.bfloat16
AF = mybir.ActivationFunctionType
ALU = mybir.AluOpType
AX = mybir.AxisListType

NCORES = 8
D = 2048
T = 8192
DEPTH = 2
NT = T // NCORES
DFF = 8192
ALPHA = (2 * DEPTH) ** 0.25
LN_EPS = 1e-5
RMS_EPS = 1e-6

ENGS = ("pe", "act", "dve", "pool", "sp")
import os as _os
SAME_ENGINE_INORDER = tuple(_os.environ.get("K_INORDER", "pe").split(","))


class Prog:
    def __init__(self, nc, n_dma_sems=12):
        self.nc = nc
        self.ops = []
        self.streams = {e: [] for e in ENGS}
        self.last_w = {}
        self.readers = {}
        self.nds = n_dma_sems
        self.dma_cnt = {e: [0] * n_dma_sems for e in ENGS + ("poolg",)}
        self.dma_rr = {e: 0 for e in ENGS + ("poolg",)}

    def _track(self, oid, reads, writes):
        deps = set()
        for k in reads:
            if k in self.last_w:
                deps.add(self.last_w[k])
        for k in writes:
            if k in self.last_w:
                deps.add(self.last_w[k])
            deps.update(self.readers.get(k, ()))
        for k in reads:
            self.readers.setdefault(k, []).append(oid)
        for k in writes:
            self.last_w[k] = oid
            self.readers[k] = []
        deps.discard(oid)
        return deps

    def op(self, eng, fn, reads=(), writes=()):
        oid = len(self.ops)
        deps = self._track(oid, reads, writes)
        self.ops.append(dict(eng=eng, fn=fn, deps=deps, dma=None, ms=None))
        self.streams[eng].append(oid)
        return oid

    def dma(self, eng, out, in_, reads=(), writes=(), **kw):
        oid = len(self.ops)
        deps = self._track(oid, reads, writes)
        si = self.dma_rr[eng]
        self.dma_rr[eng] = (si + 1) % self.nds
        self.dma_cnt[eng][si] += 16
        val = self.dma_cnt[eng][si]

        def fn(e, out=out, in_=in_, kw=kw):
            return e.dma_start(out=out, in_=in_, **kw)

        self.ops.append(dict(eng=eng, fn=fn, deps=deps, dma=(eng, si, val), ms=None))
        self.streams[eng].append(oid)
        return oid

    def cc(self, kind, ins, outs, reads=(), writes=()):
        oid = len(self.ops)
        deps = self._track(oid, reads, writes)
        idx = getattr(self, "n_cc", 0)
        self.n_cc = idx + 1

        def fn(e):
            return e.collective_compute(kind, ALU.bypass, replica_groups=[list(range(NCORES))], ins=ins, outs=outs)

        self.ops.append(dict(eng="pool", fn=fn, deps=deps, dma=("cc", idx), ms=None))
        self.streams["pool"].append(oid)
        self.barrier()
        return oid

    def gather(self, out, in2d, idx_ap, nrows, reads=(), writes=()):
        oid = len(self.ops)
        deps = self._track(oid, reads, writes)
        eng = "pool"
        GG = getattr(self, "gather_grp", "poolg")
        si = self.dma_rr[GG]
        self.dma_rr[GG] = (si + 1) % self.nds
        self.dma_cnt[GG][si] += 16
        val = self.dma_cnt[GG][si]

        def fn(e):
            if getattr(self, "_bcreg", None) is None:
                self._bcreg = e.to_reg(nrows - 1)
            return e.indirect_dma_start(out=out, out_offset=None, in_=in2d,
                                        in_offset=bass.IndirectOffsetOnAxis(ap=idx_ap, axis=0),
                                        bounds_check=self._bcreg, oob_is_err=False)

        self.ops.append(dict(eng=eng, fn=fn, deps=deps, dma=(GG, si, val), ms=None))
        self.streams[eng].append(oid)
        return oid

    def barrier(self):
        deps = set()
        for e in ENGS:
            for oid in reversed(self.streams[e]):
                if not self.ops[oid].get("noop"):
                    deps.add(oid)
                    break
        deps.update(i for i in range(getattr(self, "_bar_from", 0), len(self.ops)) if self.ops[i]["dma"] is not None)
        self._bar_from = len(self.ops)
        for e in ENGS:
            oid = len(self.ops)
            self.ops.append(dict(eng=e, fn=lambda h: None, deps=set(deps), dma=None, ms=None, noop=True))
            self.streams[e].append(oid)

    def emit(self):
        nc = self.nc
        ops = self.ops
        need = set()
        for o in ops:
            for d in o["deps"]:
                dd = ops[d]
                if dd["dma"] is None and not (dd["eng"] == o["eng"] and o["eng"] in SAME_ENGINE_INORDER):
                    need.add(d)
        cnt = {e: 0 for e in ENGS}
        for e in ENGS:
            for oid in self.streams[e]:
                if oid in need:
                    cnt[e] += 1
                    ops[oid]["ms"] = cnt[e]
        with contextlib.ExitStack() as st:
            esem = {e: st.enter_context(nc.semaphore("es_" + e)) for e in ENGS}
            dsem = {}
            for e in ENGS + ("poolg",):
                if any(c > 0 for c in self.dma_cnt[e]):
                    dsem[e] = [st.enter_context(nc.semaphore("ds_%s_%d" % (e, i))) for i in range(self.nds)]
            ccsem = [st.enter_context(nc.semaphore("cc_%d" % i)) for i in range(getattr(self, "n_cc", 0))]
            block = st.enter_context(nc.Block())

            def run_stream(e, h):
                waited = {}

                def wait(key, sem, val):
                    if val <= 0 or waited.get(key, 0) >= val:
                        return
                    h.wait_ge(sem, val)
                    waited[key] = val

                for oid in self.streams[e]:
                    o = ops[oid]
                    for d in sorted(o["deps"]):
                        dd = ops[d]
                        if dd["dma"] is not None and dd["dma"][0] == "cc":
                            wait(("cc", dd["dma"][1]), ccsem[dd["dma"][1]], 1)
                        elif dd["dma"] is not None:
                            grp, si, val = dd["dma"]
                            wait(("d", grp, si), dsem[grp][si], val)
                        else:
                            if dd["eng"] == e and e in SAME_ENGINE_INORDER:
                                continue
                            wait(("e", dd["eng"]), esem[dd["eng"]], dd["ms"])
                    if o["dma"] is not None and o["dma"][0] != "cc":
                        grp, si, val = o["dma"]
                        wait(("d", grp, si), dsem[grp][si], val - 16)
                    ins = o["fn"](h)
                    if ins is None:
                        continue
                    if o["dma"] is not None and o["dma"][0] == "cc":
                        ins.then_inc(ccsem[o["dma"][1]])
                    elif o["dma"] is not None:
                        ins.then_inc(dsem[o["dma"][0]][o["dma"][1]], 16)
                    elif o["ms"] is not None:
                        ins.then_inc(esem[e], 1)

            @block.tensor
            def _(h):
                run_stream("pe", h)

            @block.scalar
            def _(h):
                run_stream("act", h)

            @block.vector
            def _(h):
                run_stream("dve", h)

            @block.gpsimd
            def _(h):
                run_stream("pool", h)

            @block.sync
            def _(h):
                run_stream("sp", h)


class Ctx:
    def __init__(self, nc):
        self.nc = nc
        self.P = Prog(nc)
        self.st = contextlib.ExitStack()
        self.ps = [self.st.enter_context(nc.psum_tensor("ps%d" % i, [128, 512], F32)) for i in range(8)]
        self.ps_rr = 0
        self.uid = 0

    def sb(self, name, shape, dt, scope=None):
        self.uid += 1
        return (scope or self.st).enter_context(self.nc.sbuf_tensor("%s_%d" % (name, self.uid), shape, dt))

    def psum(self):
        while True:
            i = self.ps_rr
            self.ps_rr = (i + 1) % 8
            if i not in getattr(self, "reserved", ()):
                return i

    def psum_reserve(self):
        i = self.psum()
        self.reserved = set(getattr(self, "reserved", ())) | {i}
        return i

    def psum_release(self, i):
        self.reserved.discard(i)

    def mm(self, psi, out, lhsT, rhs, start, stop, reads):
        self.P.op("pe", lambda h: h.matmul(out, lhsT, rhs, start=start, stop=stop),
                  reads=reads, writes=[("ps", psi)])

    def act(self, out, in_, func, reads, writes, bias=None, scale=None):
        kw = {}
        if bias is not None:
            kw["bias"] = bias
        if scale is not None:
            kw["scale"] = scale
        self.P.op("act", lambda h: h.activation(out, in_, func, **kw), reads=reads, writes=writes)

    def tt(self, out, in0, in1, op, reads, writes, eng="dve"):
        self.P.op(eng, lambda h: h.tensor_tensor(out, in0, in1, op), reads=reads, writes=writes)

    def ts(self, out, in0, s1, s2, op0, op1, reads, writes, eng="dve"):
        if op1 is None:
            self.P.op(eng, lambda h: h.tensor_scalar(out, in0, s1, None, op0), reads=reads, writes=writes)
        else:
            self.P.op(eng, lambda h: h.tensor_scalar(out, in0, s1, s2, op0, op1), reads=reads, writes=writes)

    def stt(self, out, in0, scalar, in1, op0, op1, reads, writes):
        self.P.op("dve", lambda h: h.scalar_tensor_tensor(out, in0, scalar, in1, op0, op1),
                  reads=reads, writes=writes)

    def copy(self, out, in_, reads, writes, eng="dve"):
        if eng == "act":
            self.P.op(eng, lambda h: h.activation(out, in_, AF.Copy), reads=reads, writes=writes)
        else:
            self.P.op(eng, lambda h: h.tensor_copy(out, in_), reads=reads, writes=writes)

    def memset(self, ap, val, writes, eng="dve"):
        self.P.op(eng, lambda h: h.memset(ap, val), writes=writes)

    def recip(self, out, in_, reads, writes):
        self.P.op("dve", lambda h: h.reciprocal(out, in_), reads=reads, writes=writes)

    def load(self, out, in_, writes, eng="sp", reads=()):
        self.P.dma(eng, out, in_, reads=reads, writes=writes)

    def finish(self, keys):
        self.P.op("sp", lambda h: None, reads=keys)
        self.P.emit()
        self.st.close()


def dense_layer(C, stg, E, nt, tag):
    TB = 512
    NB = nt // TB
    P = C.P
    ps = C.ps
    xT, x1s = E["xT"], E["x1s"]
    ones = C.sb("ones", [128, 128], F32, stg)
    C.memset(ones[:], 1.0, [("ones",)])
    vt = C.sb("vecs_sb", [128, 80], F32, stg)
    C.load(vt[:], E["vecs"], [("vecs",)])
    GLU_B, SSD_NW, LN1G, LN1B, LN2G, LN2B = 0, 8, 16, 32, 48, 64
    NST = 2
    wst = C.sb("wst", [128, NST, 16 * 512], BF16, stg)
    wst_rr = [0]

    def wload(src2d, k0, nk, c0, ncols):
        s = wst_rr[0]
        wst_rr[0] = (s + 1) % NST
        view = wst[:, s, 0:nk * ncols].rearrange("p (k c) -> p k c", k=nk)
        src = src2d[k0 * 128:(k0 + nk) * 128, c0:c0 + ncols].rearrange("(k p) c -> p k c", p=128)
        C.load(view, src, [("wst", s)], eng="pool")
        return view, ("wst", s)

    NSC = 6
    sc = C.sb("sc", [128, NSC, TB], F32, stg)
    sc_rr = [0]

    def scratch():
        i = sc_rr[0]
        sc_rr[0] = (i + 1) % NSC
        return sc[:, i, :], ("sc", i)

    x1b_all = C.sb("x1b", [128, 16, nt], BF16, stg)

    def rstd_inplace(v, vk):
        C.act(v, v, AF.Sqrt, [vk], [vk])
        C.recip(v, v, [vk], [vk])

    def layer_norm_T(z, tb, gcol, bcol, dst_dram, dtag, dst_bf16, bf_dram):
        p1 = C.psum()
        p2 = C.psum()
        for f in range(16):
            C.mm(p1, ps[p1][:], ones[:], z[:, f, :], f == 0, f == 15, [("z", f), ("ones",)])
        for f in range(16):
            s_, sk = scratch()
            C.act(s_, z[:, f, :], AF.Square, [("z", f)], [sk])
            C.mm(p2, ps[p2][:], ones[:], s_, f == 0, f == 15, [sk, ("ones",)])
        mean, mk = scratch()
        C.ts(mean, ps[p1][:], 1.0 / D, None, ALU.mult, None, [("ps", p1)], [mk])
        var, vk = scratch()
        C.tt(var, mean, mean, ALU.mult, [mk], [vk])
        C.stt(var, ps[p2][:], 1.0 / D, var, ALU.mult, ALU.subtract, [("ps", p2), vk], [vk])
        C.ts(var, var, LN_EPS, None, ALU.add, None, [vk], [vk])
        rstd_inplace(var, vk)
        for f in range(16):
            zf = z[:, f, :]
            zk = ("z", f)
            C.tt(zf, zf, mean, ALU.subtract, [zk, mk], [zk])
            C.tt(zf, zf, var, ALU.mult, [zk, vk], [zk])
            C.ts(zf, zf, vt[:, gcol + f:gcol + f + 1], vt[:, bcol + f:bcol + f + 1], ALU.mult, ALU.add,
                 [zk, ("vecs",)], [zk])
            if dst_bf16 is not None:
                C.copy(dst_bf16[:, f, :], zf, [zk], [("x1b", f, tb)], eng="act")
                if bf_dram is not None:
                    C.load(bf_dram[f * 128:(f + 1) * 128, tb * TB:(tb + 1) * TB], dst_bf16[:, f, :],
                           [("dram", tag, dtag + "b", f, tb)], reads=[("x1b", f, tb)])
            C.load(dst_dram[f * 128:(f + 1) * 128, tb * TB:(tb + 1) * TB], zf, [("dram", tag, dtag, f, tb)], reads=[zk])

    for tb in range(NB):
        tsl = slice(tb * TB, (tb + 1) * TB)
        x1b = x1b_all[:, :, tsl]
        stb = contextlib.ExitStack()
        mT = C.sb("mT", [128, 16, TB], BF16, stb)
        with contextlib.ExitStack() as sa:
            xTb = C.sb("xTb", [128, 16, TB], BF16, sa)
            if E.get("xb") is not None:
                C.load(xTb[:], E["xb"][:, tsl].rearrange("(k p) t -> p k t", p=128), [("xTb", kc) for kc in range(16)])
            else:
                for kc in range(16):
                    C.load(xTb[:, kc, :], xT[kc * 128:(kc + 1) * 128, tsl], [("xTb", kc)], eng="pool")
            yT = [C.sb("yT%d" % b, [128, 8, TB], BF16, sa) for b in range(4)]
            gel = C.sb("gel", [128, 8, TB], F32, sa)
            gelb = C.sb("gelb", [128, 8, TB], BF16, sa)
            acc = C.sb("acc", [128, 4, TB], F32, sa)
            load_y = E["load_y"]
            for ct in range(8):
                a, ak = scratch()
                load_y(1, ct, tb, a, ak)
                C.copy(yT[1][:, ct, :], a, [ak], [("yT", 1, ct)], eng="act")
            for g2 in range(2):
                wv, wk = wload(E["w_rg"], 0, 16, g2 * 512, 512)
                for hh in range(2):
                    h = g2 * 2 + hh
                    yt = []
                    for j in range(2):
                        ct = 2 * h + j
                        a, ak = scratch()
                        load_y(0, ct, tb, a, ak)
                        yt.append((a, ak, ct))
                    p1 = C.psum()
                    p2 = C.psum()
                    for j, (a, ak, ct) in enumerate(yt):
                        C.mm(p1, ps[p1][:], ones[:], a, j == 0, j == 1, [ak, ("ones",)])
                    for j, (a, ak, ct) in enumerate(yt):
                        s_, sk = scratch()
                        C.act(s_, a, AF.Square, [ak], [sk])
                        C.mm(p2, ps[p2][:], ones[:], s_, j == 0, j == 1, [sk, ("ones",)])
                    mean, mk = scratch()
                    C.ts(mean, ps[p1][:], 1.0 / 256, None, ALU.mult, None, [("ps", p1)], [mk])
                    var, vk = scratch()
                    C.tt(var, mean, mean, ALU.mult, [mk], [vk])
                    C.stt(var, ps[p2][:], 1.0 / 256, var, ALU.mult, ALU.subtract, [("ps", p2), vk], [vk])
                    C.ts(var, var, LN_EPS, None, ALU.add, None, [vk], [vk])
                    rstd_inplace(var, vk)
                    for j, (a, ak, ct) in enumerate(yt):
                        C.tt(a, a, mean, ALU.subtract, [ak, mk], [ak])
                        C.tt(a, a, var, ALU.mult, [ak, vk], [ak])
                        pg = C.psum()
                        cc = (ct % 4) * 128
                        for kc in range(16):
                            C.mm(pg, ps[pg][:], wv[:, kc, cc:cc + 128], xTb[:, kc, :], kc == 0, kc == 15,
                                 [wk, ("xTb", kc)])
                        C.act(acc[:, j, :], ps[pg][:], AF.Silu, [("ps", pg)], [("acc", j)])
                        C.tt(yT[0][:, ct, :], a, acc[:, j, :], ALU.mult, [ak, ("acc", j)], [("yT", 0, ct)])
            for g in range(2):
                tl = []
                p2 = C.psum()
                for j in range(4):
                    ct = 4 * g + j
                    a = gel[:, j, :]
                    ak = ("gel", j)
                    load_y(3, ct, tb, a, ak)
                    tl.append((a, ak, ct))
                for j, (a, ak, ct) in enumerate(tl):
                    s_, sk = scratch()
                    C.act(s_, a, AF.Square, [ak], [sk])
                    C.mm(p2, ps[p2][:], ones[:], s_, j == 0, j == 3, [sk, ("ones",)])
                r_, rk = scratch()
                C.ts(r_, ps[p2][:], 1.0 / 512, RMS_EPS, ALU.mult, ALU.add, [("ps", p2)], [rk])
                rstd_inplace(r_, rk)
                for j, (a, ak, ct) in enumerate(tl):
                    C.stt(yT[3][:, ct, :], a, vt[:, SSD_NW + ct:SSD_NW + ct + 1], r_, ALU.mult, ALU.mult,
                          [ak, rk, ("vecs",)], [("yT", 3, ct)])
            for ct in range(8):
                a, ak = scratch()
                load_y(2, ct, tb, a, ak)
                C.act(gel[:, ct, :], a, AF.Gelu, [ak], [("gel", ct)])
                C.copy(gelb[:, ct, :], gel[:, ct, :], [("gel", ct)], [("gelb", ct)])
            for g2 in range(2):
                wv, wk = wload(E["glu_w"], 0, 8, g2 * 512, 512)
                for oi in range(4):
                    ot = g2 * 4 + oi
                    pg = C.psum()
                    for kc in range(8):
                        C.mm(pg, ps[pg][:], wv[:, kc, oi * 128:(oi + 1) * 128], gelb[:, kc, :], kc == 0, kc == 7,
                             [wk, ("gelb", kc)])
                    s_, sk = scratch()
                    C.act(s_, ps[pg][:], AF.Sigmoid, [("ps", pg), ("vecs",)], [sk], bias=vt[:, GLU_B + ot:GLU_B + ot + 1])
                    C.tt(yT[2][:, ot, :], s_, gel[:, ot, :], ALU.mult, [sk, ("gel", ot)], [("yT", 2, ot)])
            for fg in range(4):
                for b in range(4):
                    gv, gk = wload(E["w_gate"], 0, 16, b * D + fg * 512, 512)
                    bv, bk = wload(E["wb"][b], 0, 8, fg * 512, 512)
                    for fi in range(4):
                        f = fg * 4 + fi
                        pa = C.psum()
                        pb = C.psum()
                        for kc in range(16):
                            C.mm(pa, ps[pa][:], gv[:, kc, fi * 128:(fi + 1) * 128], xTb[:, kc, :], kc == 0, kc == 15,
                                 [gk, ("xTb", kc)])
                        for kc in range(8):
                            C.mm(pb, ps[pb][:], bv[:, kc, fi * 128:(fi + 1) * 128], yT[b][:, kc, :], kc == 0, kc == 7,
                                 [bk, ("yT", b, kc)])
                        s_, sk = scratch()
                        C.act(s_, ps[pa][:], AF.Sigmoid, [("ps", pa)], [sk])
                        ak = ("acc", fi)
                        if b == 0:
                            C.tt(acc[:, fi, :], s_, ps[pb][:], ALU.mult, [sk, ("ps", pb)], [ak])
                        else:
                            C.tt(s_, s_, ps[pb][:], ALU.mult, [sk, ("ps", pb)], [sk])
                            if b < 3:
                                C.tt(acc[:, fi, :], acc[:, fi, :], s_, ALU.add, [sk, ak], [ak])
                            else:
                                C.tt(mT[:, f, :], acc[:, fi, :], s_, ALU.add, [sk, ak], [("mT", f)])
            P.barrier()
        with contextlib.ExitStack() as sb_:
            zbuf = C.sb("zbuf", [128, 16, TB], F32, sb_)
            for fg in range(4):
                wv, wk = wload(E["w_out"], 0, 16, fg * 512, 512)
                for fi in range(4):
                    f = fg * 4 + fi
                    pa = C.psum()
                    for kc in range(16):
                        C.mm(pa, ps[pa][:], wv[:, kc, fi * 128:(fi + 1) * 128], mT[:, kc, :], kc == 0, kc == 15,
                             [wk, ("mT", kc)])
                    a, ak = scratch()
                    C.load(a, xT[f * 128:(f + 1) * 128, tsl], [ak])
                    C.stt(zbuf[:, f, :], a, ALPHA, ps[pa][:], ALU.mult, ALU.add, [ak, ("ps", pa)], [("z", f)])
            layer_norm_T(zbuf, tb, LN1G, LN1B, x1s, "x1", x1b, None)
            P.barrier()
        stb.close()
    with contextlib.ExitStack() as sc_:
        NQ = 8
        hT = C.sb("hT", [128, 8, nt], BF16, sc_)
        zacc = C.sb("zacc", [128, 16, nt], F32, sc_)
        for qd in range(NQ):
            for cg in range(2):
                wv, wk = wload(E["w_up"], 0, 16, (qd * 2 + cg) * 512, 512)
                for ci in range(4):
                    k8 = cg * 4 + ci
                    for tb in range(NB):
                        tsl = slice(tb * TB, (tb + 1) * TB)
                        pa = C.psum()
                        for kc in range(16):
                            C.mm(pa, ps[pa][:], wv[:, kc, ci * 128:(ci + 1) * 128], x1b_all[:, kc, tsl], kc == 0, kc == 15,
                                 [wk, ("x1b", kc, tb)])
                        a, ak = scratch()
                        C.act(a, ps[pa][:], AF.Relu, [("ps", pa)], [ak])
                        C.tt(hT[:, k8, tsl], a, a, ALU.mult, [ak], [("hT", k8, tb)])
            for fg in range(4):
                wv, wk = wload(E["w_down"], qd * 8, 8, fg * 512, 512)
                for fi in range(4):
                    f = fg * 4 + fi
                    for tb in range(NB):
                        tsl = slice(tb * TB, (tb + 1) * TB)
                        pa = C.psum()
                        for k in range(8):
                            C.mm(pa, ps[pa][:], wv[:, k, fi * 128:(fi + 1) * 128], hT[:, k, tsl], k == 0, k == 7,
                                 [wk, ("hT", k, tb)])
                        zk = ("zacc", f, tb)
                        if qd == 0:
                            C.copy(zacc[:, f, tsl], ps[pa][:], [("ps", pa)], [zk], eng="act")
                        else:
                            C.tt(zacc[:, f, tsl], zacc[:, f, tsl], ps[pa][:], ALU.add, [zk, ("ps", pa)], [zk])
        for tb in range(NB):
            tsl = slice(tb * TB, (tb + 1) * TB)
            zv = zacc[:, :, tsl]
            for f in range(16):
                a, ak = scratch()
                C.load(a, x1s[f * 128:(f + 1) * 128, tsl], [ak], reads=[("dram", tag, "x1", f, tb)])
                C.stt(zv[:, f, :], a, ALPHA, zv[:, f, :], ALU.mult, ALU.add, [ak, ("zacc", f, tb)], [("z", f)])
            layer_norm_T(zv, tb, LN2G, LN2B, E["out"], "out", x1b_all[:, :, tsl] if E.get("out_bf") is not None else None,
                         E.get("out_bf"))
        P.barrier()


def build_dense(nt=NT, phases=()):
    nc = bass.Bass("TRN2", target_bir_lowering=False)

    def din(name, shape):
        return nc.dram_tensor(name, shape, F32, kind="ExternalInput").ap()

    E = {"xT": din("xT", [D, nt])}
    ysrc = [din(n, [1024, nt]) for n in ("retT", "gdnT", "s5T", "ssdT")]
    E["w_gate"] = din("w_gate", [D, 4 * D])
    E["w_rg"] = din("w_rg", [D, 1024])
    E["wb"] = [din("wb%d" % b, [1024, D]) for b in range(4)]
    E["w_out"] = din("w_out", [D, D])
    E["w_up"] = din("w_up", [D, DFF])
    E["w_down"] = din("w_down", [DFF, D])
    E["glu_w"] = din("glu_w", [1024, 1024])
    E["vecs"] = din("vecs", [128, 80])
    E["out"] = nc.dram_tensor("outT", [D, nt], F32, kind="ExternalOutput").ap()
    E["x1s"] = nc.dram_tensor("x1s", [D, nt], F32, kind="Internal").ap()
    C = Ctx(nc)

    def load_y(m, ct, tb, dst, key):
        C.load(dst, ysrc[m][ct * 128:(ct + 1) * 128, tb * 512:(tb + 1) * 512], [key])

    E["load_y"] = load_y
    dense_layer(C, C.st, E, nt, "d")
    C.finish([k for k in list(C.P.last_w.keys()) if k[0] == "dram"])
    return nc


def dense_inputs(l, c, inp, xT_full, retT, gdnT, s5T, ssdT, nt=NT):
    sl = slice(c * nt, (c + 1) * nt)
    w_in = inp["w_in"][l]
    colv = lambda v, n: np.ascontiguousarray(np.asarray(v, np.float32).reshape(n, 128).T)
    vecs = np.concatenate([colv(inp["s5_glu_b"][l], 8), colv(inp["ssd_norm_w"][l], 8), colv(inp["ln1_g"][l], 16),
                           colv(inp["ln1_b"][l], 16), colv(inp["ln2_g"][l], 16), colv(inp["ln2_b"][l], 16)], axis=1)
    act = {} if xT_full is None else {
        "xT": np.ascontiguousarray(xT_full[:, sl]),
        "retT": np.ascontiguousarray(retT[:, sl]), "gdnT": np.ascontiguousarray(gdnT[:, sl]),
        "s5T": np.ascontiguousarray(s5T[:, sl]), "ssdT": np.ascontiguousarray(ssdT[:, sl])}
    return {
        **act,
        "w_gate": np.ascontiguousarray(w_in[:, 10784:]), "w_rg": np.ascontiguousarray(w_in[:, 2048:3072]),
        "wb0": inp["w_branch_ret"][l], "wb1": inp["w_branch_gdn"][l], "wb2": inp["w_branch_s5"][l],
        "wb3": inp["w_branch_ssd"][l], "w_out": inp["w_out"][l], "w_up": inp["w_up"][l], "w_down": inp["w_down"][l],
        "glu_w": inp["s5_glu_w"][l], "vecs": np.ascontiguousarray(vecs),
    }


NEG = -30000.0
_CO = {}
_off = 0
for _n, _w in [("ident", 128), ("ones", 128), ("mb_incl", 128), ("strict", 128), ("E0", 128), ("E1", 128), ("E32", 128),
               ("E64", 128), ("El127", 128), ("El63", 128), ("shift64", 128), ("cmask", 512), ("ret_dmat", 128),
               ("ret_qdec", 128), ("ret_kdec", 1), ("ret_gc", 1), ("rope_inv", 1), ("rope_sign", 1),
               ("nsel0", 1), ("nsel32", 1), ("nsel64", 1), ("sgn_pm", 1), ("sgn_mm", 1), ("tau1", 512)]:
    _CO[_n] = (_off, _w)
    _off += _w
NCONST = _off


def make_consts(c):
    k = np.zeros((128, NCONST), np.float64)

    def put(name, arr):
        o, w = _CO[name]
        k[:, o:o + w] = np.asarray(arr, np.float64).reshape(128, w)

    i = np.arange(128)
    put("ident", np.eye(128))
    put("ones", np.ones((128, 128)))
    put("mb_incl", np.where(i[None, :] >= i[:, None], 0.0, NEG))
    put("strict", np.where(i[None, :] > i[:, None], 1.0, 0.0))
    for r in (0, 1, 32, 64):
        e = np.zeros((128, 128)); e[r, :] = 1.0
        put("E%d" % r, e)
    e = np.zeros((128, 128)); e[127, :] = 1.0; put("El127", e)
    e = np.zeros((128, 128)); e[63, :] = 1.0; put("El63", e)
    sh = np.zeros((128, 128)); sh[i, (i + 64) % 128] = 1.0; put("shift64", sh)
    cm = np.ones((128, 512)); cm[:, 0::128] = 0.0; cm[0:32, 0::64] = 0.0; put("cmask", cm)
    h = c // 2
    lg = np.log(1.0 - 2.0 ** (-5.0 - h))
    s = 128.0 ** -0.5
    d = i[None, :] - i[:, None]
    put("ret_dmat", np.where(d >= 0, np.exp(np.maximum(d, 0) * lg) * s, 0.0))
    put("ret_qdec", np.broadcast_to(np.exp((i + 1.0) * lg)[None, :], (128, 128)))
    put("ret_kdec", np.exp((127.0 - i) * lg) * s)
    put("ret_gc", np.full(128, np.exp(128.0 * lg)))
    put("rope_inv", (10000.0 ** (-(np.arange(64, dtype=np.float32) / np.float32(64))).astype(np.float64))[i % 64])
    put("rope_sign", np.where(i < 64, -1.0, 1.0))
    for r in (0, 32, 64):
        v = np.zeros(128); v[r] = -1.0; put("nsel%d" % r, v)
    put("sgn_pm", np.where(i < 64, 1.0, -1.0))
    put("sgn_mm", np.full(128, -1.0))
    put("tau1", np.broadcast_to(np.arange(1, 513, dtype=np.float64)[None, :], (128, 512)))
    return k.astype(np.float32)


TWO_PI = 2.0 * np.pi
CW1 = float(np.float32(6.28125))
CW2 = float(np.float32(TWO_PI - 6.28125))


class Mix:
    def __init__(self, nt, phases, standalone=True):
        self.nt = nt
        self.TB = 512
        self.NBLK = nt // self.TB
        self.ybuf = None
        self.xg = None
        if not standalone:
            return
        nc = self.nc = bass.Bass("TRN2", target_bir_lowering=False)
        din = lambda name, shape, dt=F32: nc.dram_tensor(name, shape, dt, kind="ExternalInput").ap()
        self.xT = din("xT", [D, nt])
        self.pos = din("pos", [1, nt], mybir.dt.int32)
        self.wmix = din("wmix", [D, 15 * 128])
        self.consts_d = din("consts", [128, NCONST])
        self.prm_d = din("prm", [128, 64])
        self.nw_d = din("gdn_nw", [128, 128])
        self.s5m_d = din("s5m", [128, 4, 8, 128])
        dout = lambda name, shape: nc.dram_tensor(name, shape, F32, kind="ExternalOutput").ap()
        self.outs = [dout("o_ret", [128, nt]), dout("o_gdn", [128, nt]), dout("o_s5", [128, nt]), dout("o_ssd", [128, nt])]
        C = Ctx(nc)
        self.run(C, C.st, phases, "m")
        C.finish([k for k in list(C.P.last_w.keys()) if k[0] == "dram"])

    def run(self, C, stg, phases, tag):
        self.C = C
        self.tag = tag
        self.ps = C.ps
        self.K = C.sb("consts", [128, NCONST], F32, stg)
        C.load(self.K[:], self.consts_d, [("K",)])
        self.prm = C.sb("prm", [128, 64], F32, stg)
        C.load(self.prm[:], self.prm_d, [("prm",)])
        self.xb = C.sb("xb", [128, 16, self.TB], BF16, stg)
        self.sc = C.sb("msc", [128, 10, self.TB], F32, stg)
        self.sc_rr = 0
        self.ss = C.sb("mss", [128, 16], F32, stg)
        self.ss_rr = 0
        if "ssd" in phases:
            self.phase_ssd()
        if "ret" in phases:
            self.phase_ret()
        if "gdn" in phases:
            self.phase_gdn()
        if "s5" in phases:
            self.phase_s5()

    def out_ap(self, m, blk, c0, w):
        if self.ybuf is not None:
            return self.ybuf[m, :, blk, c0:c0 + w]
        t = blk * self.TB + c0
        return self.outs[m][:, t:t + w]

    def kc(self, name, rows=128):
        o, w = _CO[name]
        return self.K[0:rows, o:o + w]

    def scratch(self, w=None):
        i = self.sc_rr
        self.sc_rr = (i + 1) % 10
        w = w or self.TB
        return self.sc[:, i, 0:w], ("msc", i)

    def small(self):
        i = self.ss_rr
        self.ss_rr = (i + 1) % 16
        return self.ss[:, i:i + 1], ("mss", i)

    def load_x(self, blk):
        C = self.C
        if blk >= self.NBLK or self.x_loaded == blk:
            return
        self.x_loaded = blk
        nb = len(self.xbs)
        buf = self.xbs[blk % nb]
        key = ("xb", blk % nb)
        t0 = blk * self.TB
        if self.xg is not None:
            r, off = blk // 2, (blk % 2) * self.TB
            src = self.xg[r * D:(r + 1) * D, off:off + self.TB].rearrange("(k p) t -> p k t", p=128)
            C.load(buf[:], src, [key], reads=[("xg", self.tag)])
            return
        src = self.xT[:, t0:t0 + self.TB].rearrange("(k p) t -> p k t", p=128)
        C.load(buf[:], src, [key], eng="pool")

    def use_x(self, blk, stack, double):
        self.cur_blk = blk

    def begin_phase(self, stack, double):
        self.xbs = [self.xb]
        if double:
            self.xbs.append(self.C.sb("xb2", [128, 16, self.TB], BF16, stack))
        self.x_loaded = -1

    def load_w(self, wt, tiles):
        C = self.C
        for j, ti in enumerate(tiles):
            src = self.wmix[:, ti * 128:(ti + 1) * 128].rearrange("(k p) c -> p k c", p=128)
            C.load(wt[:, :, j * 128:(j + 1) * 128], src, [("wt", j)], eng="pool")

    def proj(self, wt, j):
        C = self.C
        p = C.psum()
        nb = len(self.xbs)
        xb = self.xbs[self.cur_blk % nb]
        for kc in range(16):
            C.mm(p, self.ps[p][:], wt[:, kc, j * 128:(j + 1) * 128], xb[:, kc, :], kc == 0, kc == 15,
                 [("wt", j), ("xb", self.cur_blk % nb)])
        return p

    def tr(self, in_ap, in_keys, rows, cols):
        C = self.C
        p = C.psum()
        out = self.ps[p][0:cols, 0:rows]
        ident = self.kc("ident")[0:rows, 0:rows]
        C.P.op("pe", lambda h: h.transpose(out, in_ap, ident), reads=list(in_keys) + [("K",)], writes=[("ps", p)])
        return p

    def conv_silu(self, pre, pk, wcol, bias, out, ok):
        C = self.C
        TB = self.TB
        t, tk = self.scratch()
        C.ts(t, pre[:, 0:TB], wcol(0), None, ALU.mult, None, [pk, ("prm",)], [tk])
        for k in (1, 2, 3):
            C.stt(t, pre[:, k:k + TB], wcol(k), t, ALU.mult, ALU.add, [pk, tk, ("prm",)], [tk])
        if bias is None:
            C.act(out, t, AF.Silu, [tk], [ok])
        else:
            C.act(out, t, AF.Silu, [tk, ("prm",)], [ok], bias=bias)
        C.copy(pre[:, 0:3], pre[:, TB:TB + 3], [pk], [pk])

    def rows_softplus_decay(self, psm, biascol, negA):
        C = self.C
        ps = self.ps
        xs, xk = self.scratch()
        C.ts(xs, ps[psm][:], biascol, None, ALU.add, None, [("ps", psm), ("prm",)], [xk])
        ax, ak = self.scratch()
        C.act(ax, xs, AF.Abs, [xk], [ak])
        C.act(ax, ax, AF.Exp, [ak], [ak], scale=-1.0)
        C.act(ax, ax, AF.Ln, [ak], [ak], bias=1.0)
        sp, sk = self.scratch()
        C.stt(sp, xs, 0.0, ax, ALU.max, ALU.add, [xk, ak], [sk])
        dA, dk = self.scratch()
        C.ts(dA, sp, negA, None, ALU.mult, None, [sk, ("negA",)], [dk])
        acum, ack = self.scratch()
        cm = self.kc("cmask")
        C.P.op("dve", lambda h: h.tensor_tensor_scan(acum, cm, dA, 0.0, ALU.mult, ALU.add), reads=[dk, ("K",)], writes=[ack])
        return (sp, sk), (acum, ack)

    def phase_ssd(self):
        C, ps, TB = self.C, self.ps, self.TB
        P = C.P
        with contextlib.ExitStack() as sa:
            self.begin_phase(sa, True)
            wt = C.sb("wt_ssd", [128, 16, 5 * 128], BF16, sa)
            self.load_w(wt, [10, 11, 12, 13, 14])
            pre = [C.sb("pre%d" % i, [128, 3 + TB], F32, sa) for i in range(3)]
            for i in range(3):
                C.memset(pre[i][:, 0:3], 0.0, [("pre", i)])
            S_T = C.sb("ssdS", [128, 128], F32, sa)
            C.memset(S_T[:], 0.0, [("S", 0), ("S", 1)])
            negA = C.sb("negA", [128, 1], F32, sa)
            C.act(negA[:], self.prm[:, 31:32], AF.Exp, [("prm",)], [("negA",)])
            C.ts(negA[:], negA[:], -1.0, None, ALU.mult, None, [("negA",)], [("negA",)])
            keep = C.sb("ssdkeep", [128, 5, TB], F32, sa)
            nac = C.sb("ssdnac", [128, 2, TB], F32, sa)
            NF = 4
            rT_ = C.sb("ssdrT", [128, NF * 2, 128], F32, sa)
            tmp_ = C.sb("ssdtmp", [128, NF * 10, 128], F32, sa)
            ssm = C.sb("ssdsm", [128, NF * 4], F32, sa)
            convw = lambda i: (lambda k: self.prm[:, 12 + i * 4 + k:12 + i * 4 + k + 1])
            for blk in range(self.NBLK):
                t0 = blk * TB
                self.cur_blk = blk
                self.load_x(blk)
                pz = self.proj(wt, 0)
                zs = keep[:, 0, :]
                C.act(zs, ps[pz][:], AF.Silu, [("ps", pz)], [("keep", 0)])
                for i in range(3):
                    pp = self.proj(wt, 1 + i)
                    C.copy(pre[i][:, 3:3 + TB], ps[pp][:], [("ps", pp)], [("pre", i)], eng="act")
                    self.conv_silu(pre[i], ("pre", i), convw(i), self.prm[:, 24 + i:25 + i], keep[:, 1 + i, :], ("keep", 1 + i))
                psm = self.proj(wt, 4)
                self.load_x(blk + 1)
                (sp, sk), (acum, ack) = self.rows_softplus_decay(psm, self.prm[:, 30:31], negA[:])
                rows = {0: 32, 1: 64}
                for h in range(2):
                    C.ts(nac[:, h, :], acum, self.kc("nsel%d" % rows[h]), None, ALU.mult, None, [ack, ("K",)], [("nac", h)])
                xc, Bc, Cc = keep[:, 1, :], keep[:, 2, :], keep[:, 3, :]
                def chunk(ch, sl_, blk=blk, t0=t0, sp=sp, sk=sk, acum=acum, ack=ack, xc=xc, Bc=Bc, Cc=Cc, zs=zs):
                    cs = slice(ch * 128, (ch + 1) * 128)
                    rT = rT_[:, sl_ * 2:sl_ * 2 + 2, :]
                    tmp = tmp_[:, sl_ * 10:sl_ * 10 + 10, :]
                    px = self.tr(xc[:, cs], [("keep", 1)], 128, 128)
                    pB = self.tr(Bc[:, cs], [("keep", 2)], 128, 128)
                    pT = self.tr(sp[:, cs], [sk], 128, 128)
                    C.copy(rT[:, 0, :], ps[pT][:, 0:128], [("ps", pT)], [("rT", sl_, 0)], eng="act")
                    pT2 = self.tr(acum[:, cs], [ack], 128, 128)
                    C.copy(rT[:, 1, :], ps[pT2][:, 0:128], [("ps", pT2)], [("rT", sl_, 1)], eng="act")
                    pbc = C.psum()
                    C.mm(pbc, ps[pbc][:, 0:128], self.kc("El127"), rT[:, 1, :], True, True, [("K",), ("rT", sl_, 1)])
                    xdt, Bd0, Bd1 = tmp[:, 0, :], tmp[:, 1, :], tmp[:, 2, :]
                    dl, cd = [], []
                    for h in range(2):
                        r = rows[h]
                        d_, dk_ = ssm[:, sl_ * 4 + 2 * h:sl_ * 4 + 2 * h + 1], ("ssm", sl_, 2 * h)
                        C.tt(d_, ps[pbc][:, r:r + 1], rT[:, 1, r:r + 1], ALU.subtract, [("ps", pbc), ("rT", sl_, 1)], [dk_])
                        C.act(d_, d_, AF.Exp, [dk_], [dk_])
                        c_, ck_ = ssm[:, sl_ * 4 + 2 * h + 1:sl_ * 4 + 2 * h + 2], ("ssm", sl_, 2 * h + 1)
                        C.act(c_, ps[pbc][:, r:r + 1], AF.Exp, [("ps", pbc)], [ck_])
                        dl.append((d_, dk_)); cd.append((c_, ck_))
                        C.ts(xdt[:, 64 * h:64 * h + 64], ps[px][:, 64 * h:64 * h + 64], rT[:, 0, r:r + 1], None, ALU.mult, None,
                             [("ps", px), ("rT", sl_, 0)], [("tmp", sl_, 0, h)])
                        C.ts(tmp[:, 1 + h, :], ps[pB][:, 0:128], d_, None, ALU.mult, None, [("ps", pB), dk_], [("tmp", sl_, 1 + h)])
                    yield
                    pG = C.psum()
                    C.mm(pG, ps[pG][:, 0:128], Bc[:, cs], Cc[:, cs], True, True, [("keep", 2), ("keep", 3)])
                    for h in range(2):
                        r = rows[h]
                        E = self.kc("E%d" % r)
                        pA = C.psum()
                        C.mm(pA, ps[pA][:, 0:128], E, acum[:, cs], True, True, [("K",), ack])
                        pD = C.psum()
                        C.mm(pD, ps[pD][:, 0:128], E, acum[:, cs], True, False, [("K",), ack])
                        C.mm(pD, ps[pD][:, 0:128], nac[:, h, cs], self.kc("ones"), False, True, [("K",), ("nac", h)])
                        Eb, L = tmp[:, 3 + h, :], tmp[:, 5 + h, :]
                        C.act(Eb, ps[pA][:, 0:128], AF.Exp, [("ps", pA)], [("tmp", sl_, 3 + h)])
                        C.tt(L, ps[pD][:, 0:128], self.kc("mb_incl"), ALU.add, [("ps", pD), ("K",)], [("tmp", sl_, 5 + h)])
                        C.act(L, L, AF.Exp, [("tmp", sl_, 5 + h)], [("tmp", sl_, 5 + h)])
                        C.tt(L, ps[pG][:, 0:128], L, ALU.mult, [("ps", pG), ("tmp", sl_, 5 + h)], [("tmp", sl_, 5 + h)])
                        C.tt(Eb, Cc[:, cs], Eb, ALU.mult, [("keep", 3), ("tmp", sl_, 3 + h)], [("tmp", sl_, 3 + h)])
                    yield
                    py = C.psum()
                    for h in range(2):
                        hs = slice(64 * h, 64 * h + 64)
                        C.mm(py, ps[py][hs, 0:128], xdt[:, hs], tmp[:, 5 + h, :], True, False, [("tmp", sl_, 0, h), ("tmp", sl_, 5 + h)])
                        C.mm(py, ps[py][hs, 0:128], S_T[:, hs], tmp[:, 3 + h, :], False, True, [("S", h), ("tmp", sl_, 3 + h)])
                    yo = tmp[:, 7 + (ch % 2), :]
                    yk = ("tmp", sl_, 7 + (ch % 2))
                    C.stt(yo, xc[:, cs], self.prm[:, 29:30], ps[py][:, 0:128], ALU.mult, ALU.add, [("keep", 1), ("prm",), ("ps", py)], [yk])
                    C.tt(yo, yo, zs[:, cs], ALU.mult, [yk, ("keep", 0)], [yk])
                    C.load(self.out_ap(3, blk, ch * 128, 128), yo, [("dram", self.tag, "ssd", blk, ch)], reads=[yk])
                    pS = C.psum()
                    for h in range(2):
                        hs = slice(64 * h, 64 * h + 64)
                        C.mm(pS, ps[pS][:, hs], tmp[:, 1 + h, :], xdt[:, hs], True, True, [("tmp", sl_, 1 + h), ("tmp", sl_, 0, h)])
                    for h in range(2):
                        hs = slice(64 * h, 64 * h + 64)
                        C.stt(S_T[:, hs], S_T[:, hs], cd[h][0], ps[pS][:, hs], ALU.mult, ALU.add, [("S", h), cd[h][1], ("ps", pS)], [("S", h)])

                gens = [chunk(ch, ch % NF) for ch in range(TB // 128)]
                active = list(gens)
                while active:
                    for gen in list(active):
                        try:
                            next(gen)
                        except StopIteration:
                            active.remove(gen)
            P.barrier()

    def wrap_pi(self, r, rk):
        C = self.C
        m, mk = self.scratch(r.shape[-1])
        C.P.op("dve", lambda h: h.tensor_single_scalar(m, r, float(np.pi), ALU.is_gt), reads=[rk], writes=[mk])
        C.stt(r, m, -TWO_PI, r, ALU.mult, ALU.add, [mk, rk], [rk])
        C.P.op("dve", lambda h: h.tensor_single_scalar(m, r, -float(np.pi), ALU.is_lt), reads=[rk], writes=[mk])
        C.stt(r, m, TWO_PI, r, ALU.mult, ALU.add, [mk, rk], [rk])

    def sincos(self, ang, ak, sin_out, sk, cos_out, ck, itile, ik):
        C = self.C
        w = ang.shape[-1]
        y, yk = self.scratch(w)
        C.ts(y, ang, 1.0 / TWO_PI, None, ALU.mult, None, [ak], [yk])
        C.copy(itile, y, [yk], [ik])
        C.copy(y, itile, [ik], [yk])
        C.stt(ang, y, -CW1, ang, ALU.mult, ALU.add, [yk, ak], [ak])
        C.stt(ang, y, -CW2, ang, ALU.mult, ALU.add, [yk, ak], [ak])
        self.wrap_pi(ang, ak)
        self.wrap_pi(ang, ak)
        C.act(sin_out, ang, AF.Sin, [ak], [sk])
        C.ts(ang, ang, float(np.pi / 2), None, ALU.add, None, [ak], [ak])
        self.wrap_pi(ang, ak)
        C.act(cos_out, ang, AF.Sin, [ak], [ck])

    def phase_ret(self):
        C, ps, TB = self.C, self.ps, self.TB
        P = C.P
        with contextlib.ExitStack() as sa:
            self.begin_phase(sa, True)
            wt = C.sb("wt_ret", [128, 16, 5 * 128], BF16, sa)
            self.load_w(wt, [0, 1, 2, 3, 4])
            R = C.sb("retR", [128, 128], F32, sa)
            C.memset(R[:], 0.0, [("R",)])
            tab = C.sb("rettab", [128, 2, TB], F32, sa)
            qk = C.sb("retqk", [128, 3, TB], F32, sa)
            itile = C.sb("retit", [128, TB], mybir.dt.int32, sa)
            NF = 4
            tmp_ = C.sb("rettmp", [128, NF * 6, 128], F32, sa)
            for blk in range(self.NBLK):
                t0 = blk * TB
                self.cur_blk = blk
                self.load_x(blk)
                ang, ak = self.scratch()
                C.load(ang, self.pos[0:1, t0:t0 + TB].broadcast_to([128, TB]), [ak], eng="pool")
                C.ts(ang, ang, self.kc("rope_inv"), None, ALU.mult, None, [ak, ("K",)], [ak])
                self.sincos(ang, ak, tab[:, 1, :], ("tab", 1), tab[:, 0, :], ("tab", 0), itile[:], ("it",))
                C.ts(tab[:, 1, :], tab[:, 1, :], self.kc("rope_sign"), None, ALU.mult, None, [("tab", 1), ("K",)], [("tab", 1)])
                for i in range(2):
                    pa = self.proj(wt, 2 * i)
                    pb = self.proj(wt, 2 * i + 1)
                    t1, t1k = self.scratch()
                    C.tt(t1, ps[pa][:], tab[:, 0, :], ALU.mult, [("ps", pa), ("tab", 0)], [t1k])
                    C.tt(qk[:, i, :], ps[pb][:], tab[:, 1, :], ALU.mult, [("ps", pb), ("tab", 1)], [("qk", i)])
                    C.tt(qk[:, i, :], qk[:, i, :], t1, ALU.add, [("qk", i), t1k], [("qk", i)])
                pv = self.proj(wt, 4)
                self.load_x(blk + 1)
                C.copy(qk[:, 2, :], ps[pv][:], [("ps", pv)], [("qk", 2)], eng="act")
                def chunk(ch, sl_, blk=blk, t0=t0):
                    cs = slice(ch * 128, (ch + 1) * 128)
                    tmp = tmp_[:, sl_ * 6:sl_ * 6 + 6, :]
                    pk_ = self.tr(qk[:, 1, cs], [("qk", 1)], 128, 128)
                    kd, vt_, SD, qd = tmp[:, 0, :], tmp[:, 1, :], tmp[:, 2, :], tmp[:, 3, :]
                    C.ts(kd, ps[pk_][:, 0:128], self.kc("ret_kdec"), None, ALU.mult, None, [("ps", pk_), ("K",)], [("tmp", sl_, 0)])
                    pv_ = self.tr(qk[:, 2, cs], [("qk", 2)], 128, 128)
                    C.copy(vt_, ps[pv_][:, 0:128], [("ps", pv_)], [("tmp", sl_, 1)], eng="act")
                    psc = C.psum()
                    C.mm(psc, ps[psc][:, 0:128], qk[:, 1, cs], qk[:, 0, cs], True, True, [("qk", 1), ("qk", 0)])
                    C.tt(SD, ps[psc][:, 0:128], self.kc("ret_dmat"), ALU.mult, [("ps", psc), ("K",)], [("tmp", sl_, 2)])
                    C.tt(qd, qk[:, 0, cs], self.kc("ret_qdec"), ALU.mult, [("qk", 0), ("K",)], [("tmp", sl_, 3)])
                    yield
                    py = C.psum()
                    C.mm(py, ps[py][:, 0:128], vt_, SD, True, False, [("tmp", sl_, 1), ("tmp", sl_, 2)])
                    C.mm(py, ps[py][:, 0:128], R[:], qd, False, True, [("R",), ("tmp", sl_, 3)])
                    yo = tmp[:, 4 + (ch % 2), :]
                    yk = ("tmp", sl_, 4 + (ch % 2))
                    C.copy(yo, ps[py][:, 0:128], [("ps", py)], [yk], eng="act")
                    C.load(self.out_ap(0, blk, ch * 128, 128), yo, [("dram", self.tag, "ret", blk, ch)], reads=[yk])
                    pR = C.psum()
                    C.mm(pR, ps[pR][:, 0:128], kd, vt_, True, True, [("tmp", sl_, 0), ("tmp", sl_, 1)])
                    C.stt(R[:], R[:], self.kc("ret_gc"), ps[pR][:, 0:128], ALU.mult, ALU.add, [("R",), ("K",), ("ps", pR)], [("R",)])

                active = [chunk(ch, ch % NF) for ch in range(TB // 128)]
                while active:
                    for gen in list(active):
                        try:
                            next(gen)
                        except StopIteration:
                            active.remove(gen)
            P.barrier()

    def phase_gdn(self):
        C, ps, TB = self.C, self.ps, self.TB
        P = C.P
        CH = 64
        with contextlib.ExitStack() as sa:
            self.begin_phase(sa, True)
            wt = C.sb("wt_gdn", [128, 16, 5 * 128], BF16, sa)
            self.load_w(wt, [5, 6, 7, 8, 14])
            nw = C.sb("gdn_nw", [128, 128], F32, sa)
            C.load(nw[:], self.nw_d, [("nw",)])
            pre = [C.sb("gpre%d" % i, [128, 3 + TB], F32, sa) for i in range(3)]
            for i in range(3):
                C.memset(pre[i][:, 0:3], 0.0, [("pre", i)])
            S = C.sb("gdnS", [128, 128], F32, sa)
            C.memset(S[:], 0.0, [("S",)])
            negA = C.sb("gnegA", [128, 1], F32, sa)
            C.act(negA[:], self.prm[:, 31:32], AF.Exp, [("prm",)], [("negA",)])
            C.ts(negA[:], negA[:], -1.0, None, ALU.mult, None, [("negA",)], [("negA",)])
            keep = C.sb("gkeep", [128, 6, TB], F32, sa)
            NF = 4
            G_ = C.sb("gtmp", [128, NF * 28, 128], F32, sa)
            junk_ = C.sb("gjunk", [128, NF, 128], F32, sa)
            convw = lambda i: (lambda k: self.prm[:, i * 4 + k:i * 4 + k + 1])


            for blk in range(self.NBLK):
                t0 = blk * TB
                self.cur_blk = blk
                self.load_x(blk)
                for i in range(3):
                    pp = self.proj(wt, i)
                    C.copy(pre[i][:, 3:3 + TB], ps[pp][:], [("ps", pp)], [("pre", i)], eng="act")
                    self.conv_silu(pre[i], ("pre", i), convw(i), None, keep[:, i, :], ("keep", i))
                pz = self.proj(wt, 3)
                C.act(keep[:, 3, :], ps[pz][:], AF.Silu, [("ps", pz)], [("keep", 3)])
                psm = self.proj(wt, 4)
                self.load_x(blk + 1)
                C.act(keep[:, 4, :], ps[psm][:], AF.Sigmoid, [("ps", psm)], [("keep", 4)])
                (sp, sk), (acum, ack) = self.rows_softplus_decay(psm, self.prm[:, 30:31], negA[:])
                C.ts(keep[:, 5, :], acum, self.kc("nsel0"), None, ALU.mult, None, [ack, ("K",)], [("keep", 5)])
                qc, kc_, vc, zs, sig, nac0 = (keep[:, i, :] for i in range(6))
                def chunk(ch, sl_, blk=blk, t0=t0, acum=acum, ack=ack, qc=qc, kc_=kc_, vc=vc, zs=zs, sig=sig, nac0=nac0):
                    cs = slice(ch * CH, (ch + 1) * CH)
                    junk = junk_[:, sl_, :]

                    def g(i, rows=128, cols=128):
                        return G_[0:rows, sl_ * 28 + i, 0:cols], ("g", sl_, i)

                    pT = self.tr(acum[:, cs], [ack], 128, CH)
                    acT, acTk = g(0, CH)
                    C.copy(acT, ps[pT][0:CH, 0:128], [("ps", pT)], [acTk], eng="act")
                    yield
                    pT = self.tr(sig[:, cs], [("keep", 4)], 128, CH)
                    sgT, sgTk = g(1, CH)
                    C.copy(sgT, ps[pT][0:CH, 0:128], [("ps", pT)], [sgTk], eng="act")
                    yield
                    pbc = C.psum()
                    C.mm(pbc, ps[pbc][:, 0:128], self.kc("El63", CH), acT, True, True, [("K",), acTk])
                    sm, smk = g(2)
                    C.act(sm[0:CH, 0:1], acT[:, 0:1], AF.Exp, [acTk], [smk])
                    C.tt(sm[0:CH, 1:2], sm[0:CH, 0:1], sgT[:, 1:2], ALU.mult, [smk, sgTk], [smk])
                    C.tt(sm[0:CH, 2:3], ps[pbc][0:CH, 0:1], acT[:, 0:1], ALU.subtract, [("ps", pbc), acTk], [smk])
                    C.act(sm[0:CH, 2:3], sm[0:CH, 2:3], AF.Exp, [smk], [smk])
                    C.act(sm[:, 3:4], ps[pbc][:, 0:1], AF.Exp, [("ps", pbc)], [smk])
                    yield
                    pT = self.tr(zs[:, cs], [("keep", 3)], 128, CH)
                    zt, ztk = g(3, CH)
                    C.copy(zt, ps[pT][0:CH, 0:128], [("ps", pT)], [ztk], eng="act")
                    yield
                    pT = self.tr(vc[:, cs], [("keep", 2)], 128, CH)
                    vb, vbk = g(4, CH)
                    C.ts(vb, ps[pT][0:CH, 0:128], sgT[:, 1:2], None, ALU.mult, None, [("ps", pT), sgTk], [vbk])
                    yield
                    qn, qnk = g(5, CH)
                    kn, knk = g(6, CH)
                    for (src_i, dst, dk_, col, scl) in ((0, qn, qnk, 4, 128.0 ** -0.5), (1, kn, knk, 5, 1.0)):
                        pT = self.tr(keep[:, src_i, cs], [("keep", src_i)], 128, CH)
                        C.P.op("act", lambda h, pT=pT, col=col: h.activation(junk[0:CH, :], ps[pT][0:CH, 0:128], AF.Square,
                                                                             accum_out=sm[0:CH, col:col + 1]),
                               reads=[("ps", pT)], writes=[smk, ("junk",)])
                        C.act(sm[0:CH, col:col + 1], sm[0:CH, col:col + 1], AF.Ln, [smk], [smk], bias=RMS_EPS)
                        C.act(sm[0:CH, col:col + 1], sm[0:CH, col:col + 1], AF.Exp, [smk], [smk], scale=-0.5)
                        C.ts(dst, ps[pT][0:CH, 0:128], sm[0:CH, col:col + 1], scl, ALU.mult, ALU.mult, [("ps", pT), smk], [dk_])
                        yield
                    pT = self.tr(qn, [qnk], CH, 128)
                    qnT, qnTk = g(7, 128, CH)
                    C.copy(qnT, ps[pT][:, 0:CH], [("ps", pT)], [qnTk], eng="act")
                    yield
                    pT = self.tr(kn, [knk], CH, 128)
                    knT, knTk = g(8, 128, CH)
                    C.copy(knT, ps[pT][:, 0:CH], [("ps", pT)], [knTk], eng="act")
                    yield
                    kb, kbk = g(9, CH)
                    C.ts(kb, kn, sm[0:CH, 1:2], None, ALU.mult, None, [knk, smk], [kbk])
                    kt, ktk = g(10, CH)
                    C.ts(kt, kn, sm[0:CH, 2:3], None, ALU.mult, None, [knk, smk], [ktk])
                    yield
                    pD = C.psum()
                    C.mm(pD, ps[pD][0:CH, 0:CH], self.kc("E0")[:, 0:CH], acum[:, cs], True, False, [("K",), ack])
                    C.mm(pD, ps[pD][0:CH, 0:CH], nac0[:, cs], self.kc("ones")[:, 0:CH], False, True, [("K",), ("keep", 5)])
                    Gm, Gk = g(11, CH, CH)
                    C.tt(Gm, ps[pD][0:CH, 0:CH], self.kc("mb_incl", CH)[:, 0:CH], ALU.add, [("ps", pD), ("K",)], [Gk])
                    C.act(Gm, Gm, AF.Exp, [Gk], [Gk])
                    yield
                    pE = C.psum()
                    C.mm(pE, ps[pE][:, 0:CH], self.kc("E0"), acum[:, cs], True, True, [("K",), ack])
                    qd, qdk = g(12, 128, CH)
                    C.act(qd, ps[pE][:, 0:CH], AF.Exp, [("ps", pE)], [qdk])
                    C.tt(qd, qd, qnT, ALU.mult, [qdk, qnTk], [qdk])
                    yield
                    pB = C.psum()
                    C.mm(pB, ps[pB][0:CH, 0:CH], self.kc("E1")[:, 0:CH], sig[:, cs], True, True, [("K",), ("keep", 4)])
                    pK = C.psum()
                    C.mm(pK, ps[pK][0:CH, 0:CH], knT, knT, True, True, [knTk])
                    X, Xk = g(13, CH, CH)
                    C.tt(X, ps[pK][0:CH, 0:CH], Gm, ALU.mult, [("ps", pK), Gk], [Xk])
                    C.tt(X, X, ps[pB][0:CH, 0:CH], ALU.mult, [Xk, ("ps", pB)], [Xk])
                    C.stt(X, X, -1.0, self.kc("strict", CH)[:, 0:CH], ALU.mult, ALU.mult, [Xk, ("K",)], [Xk])
                    yield
                    pQ = C.psum()
                    C.mm(pQ, ps[pQ][0:CH, 0:CH], knT, qnT, True, True, [knTk, qnTk])
                    qkT, qkTk = g(14, CH, CH)
                    C.tt(qkT, ps[pQ][0:CH, 0:CH], Gm, ALU.mult, [("ps", pQ), Gk], [qkTk])
                    yield
                    pT = self.tr(X, [Xk], CH, CH)
                    XT, XTk = g(15, CH, CH)
                    C.copy(XT, ps[pT][0:CH, 0:CH], [("ps", pT)], [XTk], eng="act")
                    Pm, Pk = g(16, CH, CH)
                    C.tt(Pm, X, self.kc("ident", CH)[:, 0:CH], ALU.add, [Xk, ("K",)], [Pk])
                    yield
                    cur = (X, Xk, XT, XTk)
                    for lev in range(1, 6):
                        Xp, Xpk, XTp, XTpk = cur
                        nXT, nXTk = g(17 + 2 * (lev % 2), CH, CH)
                        nX, nXk = g(18 + 2 * (lev % 2), CH, CH)
                        p1 = C.psum()
                        C.mm(p1, ps[p1][0:CH, 0:CH], Xp, XTp, True, True, [Xpk, XTpk])
                        C.copy(nXT, ps[p1][0:CH, 0:CH], [("ps", p1)], [nXTk], eng="act")
                        yield
                        if lev < 5:
                            p2 = C.psum()
                            C.mm(p2, ps[p2][0:CH, 0:CH], XTp, Xp, True, True, [Xpk, XTpk])
                            C.copy(nX, ps[p2][0:CH, 0:CH], [("ps", p2)], [nXk])
                            yield
                        p3 = C.psum()
                        C.mm(p3, ps[p3][0:CH, 0:CH], nXT, Pm, True, True, [nXTk, Pk])
                        C.tt(Pm, Pm, ps[p3][0:CH, 0:CH], ALU.add, [Pk, ("ps", p3)], [Pk])
                        yield
                        cur = (nX, nXk, nXT, nXTk)
                    pW = C.psum()
                    C.mm(pW, ps[pW][:, 0:CH], kb, Pm, True, True, [kbk, Pk])
                    wT, wTk = g(21, 128, CH)
                    C.copy(wT, ps[pW][:, 0:CH], [("ps", pW)], [wTk], eng="act")
                    yield
                    pU = C.psum()
                    C.mm(pU, ps[pU][0:CH, 0:128], Pm, vb, True, True, [Pk, vbk])
                    us, usk = g(22, CH)
                    C.copy(us, ps[pU][0:CH, 0:128], [("ps", pU)], [usk])
                    yield
                    pw = C.psum()
                    C.mm(pw, ps[pw][0:CH, 0:128], wT, S[:], True, True, [wTk, ("S",)])
                    vn, vnk = g(23, CH)
                    C.tt(vn, us, ps[pw][0:CH, 0:128], ALU.subtract, [usk, ("ps", pw)], [vnk])
                    po = C.psum()
                    C.mm(po, ps[po][0:CH, 0:128], qd, S[:], True, False, [qdk, ("S",)])
                    C.mm(po, ps[po][0:CH, 0:128], qkT, vn, False, True, [qkTk, vnk])
                    pS = C.psum()
                    C.mm(pS, ps[pS][:, 0:128], kt, vn, True, True, [ktk, vnk])
                    C.stt(S[:], S[:], sm[:, 3:4], ps[pS][:, 0:128], ALU.mult, ALU.add, [("S",), smk, ("ps", pS)], [("S",)])
                    C.P.op("act", lambda h, po=po: h.activation(junk[0:CH, :], ps[po][0:CH, 0:128], AF.Square, accum_out=sm[0:CH, 6:7]),
                           reads=[("ps", po)], writes=[smk, ("junk",)])
                    C.act(sm[0:CH, 6:7], sm[0:CH, 6:7], AF.Ln, [smk], [smk], bias=RMS_EPS, scale=1.0 / 128)
                    C.act(sm[0:CH, 6:7], sm[0:CH, 6:7], AF.Exp, [smk], [smk], scale=-0.5)
                    on, onk = g(24 + (ch % 2), CH)
                    C.stt(on, ps[po][0:CH, 0:128], sm[0:CH, 6:7], nw[0:CH, :], ALU.mult, ALU.mult, [("ps", po), smk, ("nw",)], [onk])
                    C.tt(on, on, zt, ALU.mult, [onk, ztk], [onk])
                    yield
                    pT = self.tr(on, [onk], CH, 128)
                    onT, onTk = g(26 + (ch % 2), 128, CH)
                    C.copy(onT, ps[pT][:, 0:CH], [("ps", pT)], [onTk], eng="act")
                    C.load(self.out_ap(1, blk, ch * CH, CH), onT, [("dram", self.tag, "gdn", blk, ch)], reads=[onTk])

                gens = [chunk(ch, ch % NF) for ch in range(TB // CH)]
                for g0 in range(0, len(gens), NF):
                    active = gens[g0:g0 + NF]
                    while active:
                        for gen in list(active):
                            try:
                                next(gen)
                            except StopIteration:
                                active.remove(gen)
            P.barrier()

    def phase_s5(self):
        C, ps, TB = self.C, self.ps, self.TB
        P = C.P
        NG = 8
        with contextlib.ExitStack() as sa:
            self.begin_phase(sa, True)
            wt = C.sb("wt_s5", [128, 16, 128], BF16, sa)
            self.load_w(wt, [9])
            sm_ = C.sb("s5m", [128, 4, NG, 128], F32, sa)
            C.load(sm_[:], self.s5m_d, [("s5m",)])
            q = C.sb("s5q", [128, 16, NG], F32, sa)
            qi = C.sb("s5qi", [128, NG], mybir.dt.int32, sa)
            tabs = C.sb("s5tab", [128, NG, 4, TB], F32, sa)
            Rg = C.sb("s5R", [128, NG, 128], F32, sa)
            carry = C.sb("s5carry", [128, NG], F32, sa)
            uT = C.sb("s5u", [128, TB], F32, sa)
            itile = C.sb("s5it", [128, TB], mybir.dt.int32, sa)
            NF = 4
            gsc = C.sb("s5gsc", [128, NF, 4, TB], F32, sa)
            C.memset(carry[:], 0.0, [("carry", g) for g in range(NG)])
            lre, lim, lst = self.prm[:, 33:41], self.prm[:, 41:49], self.prm[:, 49:57]
            QK = ("q",)
            STEP, MAG, TH, COS, SIN, ABR, ABI, DEN, FRE, FIM, T0, T1, F2, G2, SP = range(15)
            qq = lambda i: q[:, i, :]
            C.act(qq(STEP), lst, AF.Exp, [("prm",)], [QK])
            C.tt(qq(MAG), lre, qq(STEP), ALU.mult, [("prm",), QK], [QK])
            C.act(qq(MAG), qq(MAG), AF.Exp, [QK], [QK])
            C.tt(qq(TH), lim, qq(STEP), ALU.mult, [("prm",), QK], [QK])
            C.copy(qq(T0), qq(TH), [QK], [QK])
            self.sincos(qq(T0), QK, qq(SIN), QK, qq(COS), QK, qi[:], ("qi",))
            C.tt(qq(ABR), qq(MAG), qq(COS), ALU.mult, [QK], [QK])
            C.tt(qq(ABI), qq(MAG), qq(SIN), ALU.mult, [QK], [QK])
            C.tt(qq(DEN), lre, lre, ALU.mult, [("prm",)], [QK])
            C.tt(qq(T0), lim, lim, ALU.mult, [("prm",)], [QK])
            C.tt(qq(DEN), qq(DEN), qq(T0), ALU.add, [QK], [QK])
            C.recip(qq(DEN), qq(DEN), [QK], [QK])
            C.ts(qq(T0), qq(ABR), -1.0, None, ALU.add, None, [QK], [QK])
            C.tt(qq(FRE), qq(T0), lre, ALU.mult, [QK, ("prm",)], [QK])
            C.tt(qq(T1), qq(ABI), lim, ALU.mult, [QK, ("prm",)], [QK])
            C.tt(qq(FRE), qq(FRE), qq(T1), ALU.add, [QK], [QK])
            C.tt(qq(FRE), qq(FRE), qq(DEN), ALU.mult, [QK], [QK])
            C.tt(qq(FIM), qq(ABI), lre, ALU.mult, [QK, ("prm",)], [QK])
            C.tt(qq(T1), qq(T0), lim, ALU.mult, [QK, ("prm",)], [QK])
            C.tt(qq(FIM), qq(FIM), qq(T1), ALU.subtract, [QK], [QK])
            C.tt(qq(FIM), qq(FIM), qq(DEN), ALU.mult, [QK], [QK])
            C.ts(qq(F2), qq(FIM), self.kc("rope_sign"), None, ALU.mult, None, [QK, ("K",)], [QK])
            C.ts(qq(G2), qq(FRE), self.kc("sgn_pm"), None, ALU.mult, None, [QK, ("K",)], [QK])
            for g in range(NG):
                cosT, sinT, TA, TBt = (tabs[:, g, i, :] for i in range(4))
                tk = lambda i: ("tab", g, i)
                ang, ak = self.scratch()
                C.ts(ang, self.kc("tau1"), q[:, TH, g:g + 1], None, ALU.mult, None, [("K",), QK], [ak])
                self.sincos(ang, ak, sinT, tk(1), cosT, tk(0), itile[:], ("it",))
                C.ts(TA, cosT, q[:, FRE, g:g + 1], None, ALU.mult, None, [tk(0), QK], [tk(2)])
                C.stt(TA, sinT, q[:, FIM, g:g + 1], TA, ALU.mult, ALU.add, [tk(1), tk(2), QK], [tk(2)])
                C.ts(TBt, cosT, q[:, F2, g:g + 1], None, ALU.mult, None, [tk(0), QK], [tk(3)])
                C.stt(TBt, sinT, q[:, G2, g:g + 1], TBt, ALU.mult, ALU.add, [tk(1), tk(3), QK], [tk(3)])
                C.ts(q[:, SP, g:g + 1], sinT[:, TB - 1:TB], self.kc("sgn_pm"), None, ALU.mult, None, [tk(1), ("K",)], [QK])
                C.ts(Rg[:, g, :], self.kc("ident"), cosT[:, TB - 1:TB], None, ALU.mult, None, [tk(0), ("K",)], [("Rg", g)])
                C.stt(Rg[:, g, :], self.kc("shift64"), q[:, SP, g:g + 1], Rg[:, g, :], ALU.mult, ALU.add, [("K",), QK, ("Rg", g)], [("Rg", g)])
                C.ts(sm_[:, 2, g, :], sm_[:, 2, g, :], self.kc("sgn_pm"), None, ALU.mult, None, [("s5m",), ("K",)], [("s5m",)])
                C.ts(sm_[:, 3, g, :], sm_[:, 3, g, :], -1.0, None, ALU.mult, None, [("s5m",)], [("s5m",)])
            for blk in range(self.NBLK):
                t0 = blk * TB
                self.cur_blk = blk
                self.load_x(blk)
                pu = self.proj(wt, 0)
                self.load_x(blk + 1)
                C.copy(uT[:], ps[pu][:], [("ps", pu)], [("uT",)], eng="act")
                py = C.psum_reserve()
                def grp(g, sl_, py=py):
                    cosT, sinT, TA, TBt = (tabs[:, g, i, :] for i in range(4))
                    tk = lambda i: ("tab", g, i)
                    v, vk = gsc[:, sl_, 0, :], ("gsc", sl_, 0)
                    v2, v2k = gsc[:, sl_, 1, :], ("gsc", sl_, 1)
                    w, wk = gsc[:, sl_, 2, :], ("gsc", sl_, 2)
                    z1, z1k = gsc[:, sl_, 3, :], ("gsc", sl_, 3)
                    p1 = C.psum()
                    C.mm(p1, ps[p1][:], sm_[:, 0, g, :], uT[:], True, True, [("s5m",), ("uT",)])
                    p2 = C.psum()
                    C.mm(p2, ps[p2][:], sm_[:, 1, g, :], uT[:], True, True, [("s5m",), ("uT",)])
                    C.tt(v, ps[p1][:], TA, ALU.mult, [("ps", p1), tk(2)], [vk])
                    C.tt(v2, ps[p2][:], TBt, ALU.mult, [("ps", p2), tk(3)], [v2k])
                    yield
                    C.tt(v, v, v2, ALU.add, [vk, v2k], [vk], eng="pool")
                    yield
                    mb = q[:, MAG, g:g + 1].broadcast_to([128, TB])
                    C.P.op("dve", lambda h, w=w, mb=mb, v=v, g=g: h.tensor_tensor_scan(w, mb, v, carry[:, g:g + 1], ALU.mult, ALU.add),
                           reads=[vk, QK, ("carry", g)], writes=[wk])
                    yield
                    C.tt(z1, w, cosT, ALU.mult, [wk, tk(0)], [z1k])
                    C.tt(v2, w, sinT, ALU.mult, [wk, tk(1)], [v2k], eng="pool")
                    yield
                    C.mm(py, ps[py][:], sm_[:, 2, g, :], z1, g == 0, False, [("s5m",), z1k])
                    C.mm(py, ps[py][:], sm_[:, 3, g, :], v2, False, g == NG - 1, [("s5m",), v2k])
                    pc = C.psum()
                    C.mm(pc, ps[pc][:, 0:1], Rg[:, g, :], w[:, TB - 1:TB], True, True, [("Rg", g), wk])
                    C.copy(carry[:, g:g + 1], ps[pc][:, 0:1], [("ps", pc)], [("carry", g)], eng="act")
                    yield

                gens = [grp(g, g % NF) for g in range(NG)]
                for g0 in range(0, NG, NF):
                    active = gens[g0:g0 + NF]
                    while active:
                        for gen in list(active):
                            try:
                                next(gen)
                            except StopIteration:
                                active.remove(gen)
                yo, yk = self.scratch()
                C.stt(yo, uT[:], self.prm[:, 32:33], ps[py][:], ALU.mult, ALU.add, [("uT",), ("prm",), ("ps", py)], [yk])
                C.psum_release(py)
                C.load(self.out_ap(2, blk, 0, TB), yo, [("dram", self.tag, "s5", blk)], reads=[yk])
            P.barrier()


def mix_inputs(l, c, inp, xT_full, nt):
    w = inp["w_in"][l]
    h, e = c // 2, c % 2
    Z = np.zeros((D, 128), np.float32)
    q = w[:, h * 128:(h + 1) * 128]
    k = w[:, 512 + h * 128:512 + (h + 1) * 128]
    perm = lambda a: np.concatenate([a[:, 64:], a[:, :64]], axis=1)
    v = w[:, 1024 + h * 256 + e * 128:1024 + h * 256 + (e + 1) * 128]
    g0 = 3072
    gq = w[:, g0 + c * 128:g0 + (c + 1) * 128]
    gk = w[:, g0 + 1024 + c * 128:g0 + 1024 + (c + 1) * 128]
    gv = w[:, g0 + 2048 + c * 128:g0 + 2048 + (c + 1) * 128]
    gz = w[:, g0 + 3072 + c * 128:g0 + 3072 + (c + 1) * 128]
    ga = w[:, g0 + 4096 + c]
    gb = w[:, g0 + 4104 + c]
    s0 = g0 + 4112
    su = w[:, s0 + c * 128:s0 + (c + 1) * 128]
    m0 = s0 + 1024
    mz = w[:, m0 + c * 128:m0 + (c + 1) * 128]
    mx = w[:, m0 + 1024 + c * 128:m0 + 1024 + (c + 1) * 128]
    grp = c // 4
    mB = w[:, m0 + 2048 + grp * 128:m0 + 2048 + (grp + 1) * 128]
    mC = w[:, m0 + 2304 + grp * 128:m0 + 2304 + (grp + 1) * 128]
    mdt = w[:, m0 + 2560 + 2 * c:m0 + 2560 + 2 * c + 2]
    small = Z.copy()
    small[:, 0] = ga; small[:, 1] = gb; small[:, 32] = mdt[:, 0]; small[:, 64] = mdt[:, 1]
    wmix = np.concatenate([q, perm(q), k, perm(k), v, gq, gk, gv, gz, su, mz, mx, mB, mC, small], axis=1)
    prm = np.zeros((128, 64), np.float32)
    gcw = inp["gdn_conv_w"][l]
    for i in range(3):
        prm[:, i * 4:(i + 1) * 4] = gcw[i * 1024 + c * 128:i * 1024 + (c + 1) * 128]
    scw = inp["ssd_conv_w"][l]
    scb = inp["ssd_conv_b"][l]
    rowsl = [slice(c * 128, (c + 1) * 128), slice(1024 + grp * 128, 1024 + (grp + 1) * 128),
             slice(1280 + grp * 128, 1280 + (grp + 1) * 128)]
    for i in range(3):
        prm[:, 12 + i * 4:12 + (i + 1) * 4] = scw[rowsl[i]]
        prm[:, 24 + i] = scb[rowsl[i]]
    prm[:64, 29] = inp["ssd_d"][l][2 * c]; prm[64:, 29] = inp["ssd_d"][l][2 * c + 1]
    prm[0, 30] = inp["gdn_dt_bias"][l][c]; prm[32, 30] = inp["ssd_dt_bias"][l][2 * c]; prm[64, 30] = inp["ssd_dt_bias"][l][2 * c + 1]
    prm[0, 31] = inp["gdn_a_log"][l][c]; prm[32, 31] = inp["ssd_a_log"][l][2 * c]; prm[64, 31] = inp["ssd_a_log"][l][2 * c + 1]
    prm[:, 32] = inp["s5_d"][l][c * 128:(c + 1) * 128]
    gs = slice(8 * c, 8 * c + 8)
    dup = lambda a: np.concatenate([a, a], axis=0)
    prm[:, 33:41] = dup(inp["s5_lam_re"][l][gs].T)
    prm[:, 41:49] = dup(inp["s5_lam_im"][l][gs].T)
    prm[:, 49:57] = np.broadcast_to(inp["s5_log_step"][l][gs][None, :], (128, 8))
    nw = np.ascontiguousarray(np.broadcast_to(inp["gdn_norm_w"][l][None, :], (128, 128))).astype(np.float32)
    s5m = np.zeros((128, 4, 8, 128), np.float32)
    bre, bim = inp["s5_b_re"][l][gs], inp["s5_b_im"][l][gs]
    cre, cim = inp["s5_c_re"][l][gs], inp["s5_c_im"][l][gs]
    for g in range(8):
        r = slice(16 * g, 16 * g + 16)
        s5m[r, 0, g, 0:64] = bre[g].T; s5m[r, 0, g, 64:128] = bim[g].T
        s5m[r, 1, g, 0:64] = bim[g].T; s5m[r, 1, g, 64:128] = bre[g].T
        s5m[0:64, 2, g, r] = cre[g].T; s5m[64:128, 2, g, r] = cim[g].T
        s5m[0:64, 3, g, r] = cim[g].T; s5m[64:128, 3, g, r] = cre[g].T
    if xT_full is None:
        return {"wmix": np.ascontiguousarray(wmix), "prm": prm, "gdn_nw": nw, "s5m": s5m}
    return {"xT": np.ascontiguousarray(xT_full[:, :nt]), "pos": np.ascontiguousarray(inp["positions"][:, :nt]).astype(np.int32),
            "wmix": np.ascontiguousarray(wmix), "consts": make_consts(c), "prm": prm, "gdn_nw": nw, "s5m": s5m}


YROWS = 4 * 128 * 16


def build_fused(debug=None):
    nc = bass.Bass("TRN2", target_bir_lowering=False)
    I32 = mybir.dt.int32
    din = lambda name, shape, dt=F32: nc.dram_tensor(name, shape, dt, kind="ExternalInput").ap()
    xT_in = din("xT", [D, NT])
    pos = din("pos", [1, T], I32)
    consts_d = din("consts", [128, NCONST])
    gidx_d = din("gidx", [128, 64], I32)
    L = []
    for l in range(DEPTH):
        s = str(l)
        if debug == "mix":
            if l == 0:
                L.append(dict(wmix=din("wmix" + s, [D, 15 * 128]), prm=din("prm" + s, [128, 64]), nw=din("gdn_nw" + s, [128, 128]),
                              s5m=din("s5m" + s, [128, 4, 8, 128])))
            continue
        L.append(dict(
            wmix=din("wmix" + s, [D, 15 * 128]), prm=din("prm" + s, [128, 64]), nw=din("gdn_nw" + s, [128, 128]),
            s5m=din("s5m" + s, [128, 4, 8, 128]), w_gate=din("w_gate" + s, [D, 4 * D]), w_rg=din("w_rg" + s, [D, 1024]),
            wb=[din("wb%d_%s" % (b, s), [1024, D]) for b in range(4)], w_out=din("w_out" + s, [D, D]),
            w_up=din("w_up" + s, [D, DFF]), w_down=din("w_down" + s, [DFF, D]), glu_w=din("glu_w" + s, [1024, 1024]),
            vecs=din("vecs" + s, [128, 80])))
    outT = nc.dram_tensor("outT", [D, NT], F32, kind="ExternalOutput").ap()
    dbgT = nc.dram_tensor("dbgT", [4, 1024, NT], F32, kind="ExternalOutput").ap() if debug == "mix" else None
    xb_loc = [nc.dram_tensor("xb_loc%d" % l, [D, NT], BF16) for l in range(DEPTH)]
    xg = [nc.dram_tensor("xg%d" % l, [NCORES * D, NT], BF16) for l in range(DEPTH)]
    ybuf = [nc.dram_tensor("ybuf%d" % l, [YROWS, 512], F32) for l in range(DEPTH)]
    yg = [nc.dram_tensor("yg%d" % l, [NCORES * YROWS, 512], F32) for l in range(DEPTH)]
    xres = nc.dram_tensor("xres", [D, NT], F32).ap()
    x1s = nc.dram_tensor("x1s", [D, NT], F32).ap()

    C = Ctx(nc)
    P = C.P
    gidx = C.sb("gidx", [128, 64], I32)
    C.load(gidx[:], gidx_d, [("gidx",)])
    with contextlib.ExitStack() as s0:
        t_ = C.sb("x0b", [128, 16, 512], BF16, s0)
        for tb in range(NT // 512):
            C.load(t_[:], xT_in[:, tb * 512:(tb + 1) * 512].rearrange("(k p) t -> p k t", p=128), [("x0b",)], eng="pool")
            C.load(xb_loc[0].ap()[:, tb * 512:(tb + 1) * 512].rearrange("(k p) t -> p k t", p=128), t_[:],
                   [("dram", "xb0", tb)], reads=[("x0b",)])
        P.barrier()
    for l in range(DEPTH):
        tag = "L%d" % l
        P.cc("AllGather", [xb_loc[l].ap().opt()], [xg[l].ap().opt()], writes=[("xg", tag)])
        with contextlib.ExitStack() as stg:
            mx = Mix(T, (), standalone=False)
            mx.xg = xg[l].ap()
            mx.ybuf = ybuf[l].ap().rearrange("(m p b) t -> m p b t", m=4, p=128)
            mx.pos = pos
            mx.wmix, mx.consts_d, mx.prm_d, mx.nw_d, mx.s5m_d = L[l]["wmix"], consts_d, L[l]["prm"], L[l]["nw"], L[l]["s5m"]
            mx.run(C, stg, ("ssd", "ret", "gdn", "s5"), tag)
            P.barrier()
        P.cc("AllGather", [ybuf[l].ap().opt()], [yg[l].ap().opt()], writes=[("yg", tag)])
        if debug == "mix":
            with contextlib.ExitStack() as stg:
                dt_ = C.sb("dbgt", [128, 4, 512], F32, stg)
                i_ = 0
                for m in range(4):
                    for ct in range(8):
                        for tb in range(2):
                            col = (ct * 4 + m) * 2 + tb
                            P.gather(dt_[:, i_ % 4, :], yg[l].ap()[:, :], gidx[:, col:col + 1], NCORES * YROWS,
                                     reads=[("yg", tag), ("gidx",)], writes=[("dbgt", i_ % 4)])
                            C.load(dbgT[m, ct * 128:(ct + 1) * 128, tb * 512:(tb + 1) * 512], dt_[:, i_ % 4, :],
                                   [("dram", "dbg", m, ct, tb)], reads=[("dbgt", i_ % 4)])
                            i_ += 1
                P.barrier()
            break
        with contextlib.ExitStack() as stg:
            E = dict(L[l])
            E["xT"] = xT_in if l == 0 else xres
            E["xb"] = xb_loc[l].ap()
            E["x1s"] = x1s
            last = l == DEPTH - 1
            E["out"] = outT if last else xres
            E["out_bf"] = None if last else xb_loc[l + 1].ap()
            yg2 = yg[l].ap()

            def load_y(m, ct, tb, dst, key, yg2=yg2, tag=tag):
                col = (ct * 4 + m) * 2 + tb
                P.gather(dst, yg2[:, :], gidx[:, col:col + 1], NCORES * YROWS, reads=[("yg", tag), ("gidx",)], writes=[key])

            E["load_y"] = load_y
            dense_layer(C, stg, E, NT, tag)
            P.barrier()
    C.finish([k for k in list(P.last_w.keys()) if k[0] == "dram"])
    return nc


def fused_inputs(c, inp):
    xT = np.ascontiguousarray(inp["x"][0].T[:, c * NT:(c + 1) * NT])
    m = {"xT": xT, "pos": np.ascontiguousarray(inp["positions"]).astype(np.int32), "consts": make_consts(c)}
    p = np.arange(128)
    gidx = np.zeros((128, 64), np.int32)
    for r in range(NCORES):
        for mm_ in range(4):
            for tb in range(2):
                gidx[:, (r * 4 + mm_) * 2 + tb] = r * YROWS + (mm_ * 128 + p) * 16 + (c * 2 + tb)
    m["gidx"] = gidx
    return m


def fused_layer_inputs(l, c, inp, shared):
    s = str(l)
    mi = mix_inputs(l, c, inp, None, T)
    m = {"wmix" + s: mi["wmix"], "prm" + s: mi["prm"], "gdn_nw" + s: mi["gdn_nw"], "s5m" + s: mi["s5m"]}
    if l not in shared:
        d = dense_inputs(l, 0, inp, None, None, None, None, None)
        shared[l] = {"w_gate" + s: d["w_gate"], "w_rg" + s: d["w_rg"], "w_out" + s: d["w_out"], "w_up" + s: d["w_up"],
                     "w_down" + s: d["w_down"], "glu_w" + s: d["glu_w"], "vecs" + s: d["vecs"],
                     **{"wb%d_%s" % (b, s): d["wb%d" % b] for b in range(4)}}
    m.update(shared[l])
    return m


_PROGS = {}


def _get_progs():
    if "mix" not in _PROGS:
        _PROGS["mix"] = Mix(T, ("ssd", "ret", "gdn", "s5")).nc
        _PROGS["dense"] = build_dense(NT)
    return _PROGS["mix"], _PROGS["dense"]


def kernel_unfused(**inputs):
    inp = {k: np.asarray(v) for k, v in inputs.items()}
    ncm, ncd = _get_progs()
    cores = list(range(NCORES))
    xT = np.ascontiguousarray(inp["x"][0].T)
    for l in range(DEPTH):
        maps = [mix_inputs(l, c, inp, xT, T) for c in cores]
        res = run_bass_kernel_spmd(ncm, maps, core_ids=cores).results
        retT = np.concatenate([res[c]["o_ret"] for c in cores], axis=0)
        gdnT = np.concatenate([res[c]["o_gdn"] for c in cores], axis=0)
        s5T = np.concatenate([res[c]["o_s5"] for c in cores], axis=0)
        ssdT = np.concatenate([res[c]["o_ssd"] for c in cores], axis=0)
        shared = dense_inputs(l, 0, inp, xT, retT, gdnT, s5T, ssdT)
        maps = []
        for c in cores:
            m = dict(shared)
            sl = slice(c * NT, (c + 1) * NT)
            m["xT"] = np.ascontiguousarray(xT[:, sl])
            m["retT"] = np.ascontiguousarray(retT[:, sl]); m["gdnT"] = np.ascontiguousarray(gdnT[:, sl])
            m["s5T"] = np.ascontiguousarray(s5T[:, sl]); m["ssdT"] = np.ascontiguousarray(ssdT[:, sl])
            maps.append(m)
        res = run_bass_kernel_spmd(ncd, maps, core_ids=cores).results
        xT = np.concatenate([res[c]["outT"] for c in cores], axis=1)
    return np.ascontiguousarray(xT.T)[None].astype(np.float32)


def kernel(**inputs):
    inp = {k: np.asarray(v) for k, v in inputs.items()}
    if "fused" not in _PROGS:
        _PROGS["fused"] = build_fused()
    nc = _PROGS["fused"]
    cores = list(range(NCORES))
    shared = {}
    maps = []
    for c in cores:
        m = fused_inputs(c, inp)
        for l in range(DEPTH):
            m.update(fused_layer_inputs(l, c, inp, shared))
        maps.append(m)
    res = run_bass_kernel_spmd(nc, maps, core_ids=cores).results
    xT = np.concatenate([res[c]["outT"] for c in cores], axis=1)
    return np.ascontiguousarray(xT.T)[None].astype(np.float32)
```
